# Optimizing a Trainium2 kernel written in Bass

```python
import jax, jax.numpy as jnp
from jax import lax
import numpy as np

D_MODEL = 1024
BATCH = 4
SEQ = 8192
DEPTH = 1
DEC_BATCH = 32
DEC_SEQ = 64
PAST_LEN = 2048

CHUNK = 64
Q_BLOCK = 128
MIX_WIDTH = D_MODEL
SB_HEADS = 8
SB_HEAD_DIM = 64
SB_WIDTH = SB_HEADS * SB_HEAD_DIM
RET_HEADS = 4
RET_HEAD_DIM = 128
RET_WIDTH = RET_HEADS * RET_HEAD_DIM
IN_WIDTH = 3 * SB_WIDTH + 4 * RET_WIDTH
D_FF = 2816
N_MOD = 9
ROPE_BASE = 10000.0
NORM_EPS = 1e-6
MACARON_WEIGHT = 0.5

kernel_name = 'hymba_stickbreak_retention_macaron_step'


def rms_norm(x, gain):
    xf = x.astype(jnp.float32)
    inv = lax.rsqrt(jnp.mean(xf * xf, axis=-1, keepdims=True) + NORM_EPS)
    return (xf * inv * gain.astype(jnp.float32)).astype(x.dtype)


def modulate(x, gain, shift, scale):
    return rms_norm(x, gain) * (1 + scale[:, None, :]) + shift[:, None, :]


def swiglu_ffn(h, w_up, w_down):
    gate, up = jnp.split(h @ w_up, 2, axis=-1)
    return (jax.nn.silu(gate) * up) @ w_down


def rope(x, pos):
    half = x.shape[-1] // 2
    inv_freq = ROPE_BASE ** (-jnp.arange(half, dtype=jnp.float32) / half)
    ang = pos.astype(jnp.float32)[:, None] * inv_freq[None, :]
    cos = jnp.cos(ang)[None, :, None, :]
    sin = jnp.sin(ang)[None, :, None, :]
    x1, x2 = x[..., :half], x[..., half:]
    return jnp.concatenate([x1 * cos - x2 * sin, x1 * sin + x2 * cos], axis=-1)


def head_group_norm(y):
    mu = jnp.mean(y, axis=-1, keepdims=True)
    yc = y - mu
    return yc * lax.rsqrt(jnp.mean(yc * yc, axis=-1, keepdims=True) + NORM_EPS)


def retention_log_decay():
    return jnp.log1p(-jnp.exp2(-5.0 - jnp.arange(RET_HEADS, dtype=jnp.float32)))


def stick_breaking(q, k, v, q_pos, k_pos):
    z = jnp.einsum('bqhd,bkhd->bhqk', q.astype(jnp.float32), k.astype(jnp.float32)) * SB_HEAD_DIM ** -0.5
    mask = k_pos[None, :] < q_pos[:, None]
    log_rest = jnp.where(mask, jax.nn.log_sigmoid(-z), 0.0)
    between = lax.cumsum(log_rest, axis=3, reverse=True) - log_rest
    weights = jnp.where(mask, jnp.exp(jax.nn.log_sigmoid(z) + between), 0.0)
    return jnp.einsum('bhqk,bkhd->bqhd', weights, v.astype(jnp.float32))


def stick_breaking_blocked(q, k, v, pos):
    B, T, H, d = q.shape
    nb = T // Q_BLOCK
    qb = q.reshape(B, nb, Q_BLOCK, H, d).transpose(1, 0, 2, 3, 4)
    pb = pos.reshape(nb, Q_BLOCK)
    out = lax.map(lambda a: stick_breaking(a[0], k, v, a[1], pos), (qb, pb))
    return out.transpose(1, 0, 2, 3, 4).reshape(B, T, H, d)


def retention(q, k, v, s0):
    B, T, H, dk = q.shape
    dv = v.shape[-1]
    c = min(CHUNK, T)
    n = T // c
    lg = retention_log_decay()
    idx = jnp.arange(c, dtype=jnp.float32)
    diff = idx[:, None] - idx[None, :]
    intra_decay = jnp.where(diff[None] >= 0,
                            jnp.exp(jnp.maximum(diff, 0.0)[None] * lg[:, None, None]), 0.0)
    q_decay = jnp.exp((idx + 1.0)[:, None] * lg[None, :])
    k_decay = jnp.exp((c - 1.0 - idx)[:, None] * lg[None, :])
    chunk_decay = jnp.exp(c * lg)
    qc = q.reshape(B, n, c, H, dk)
    kc = k.reshape(B, n, c, H, dk)
    vc = v.reshape(B, n, c, H, dv)
    scores = jnp.einsum('bnihd,bnjhd->bnhij', qc, kc) * intra_decay
    intra = jnp.einsum('bnhij,bnjhe->bnihe', scores, vc)
    kv = jnp.einsum('bnjhd,bnjhe->nbhde', kc * k_decay[:, :, None], vc)
    q_in = (qc * q_decay[:, :, None]).transpose(1, 0, 2, 3, 4)

    def step(S, xs):
        q_n, kv_n = xs
        cross = jnp.einsum('bihd,bhde->bihe', q_n, S)
        return chunk_decay[None, :, None, None] * S + kv_n, cross

    s_final, cross = lax.scan(step, s0.astype(jnp.float32), (q_in, kv))
    out = intra + cross.transpose(1, 0, 2, 3, 4)
    return out.reshape(B, T, H, dv), s_final


def trunk_layer(x, c, pos, sb_k_past, sb_v_past, ret_s0,
                w_ada, b_ada, norm_ffn1, norm_mix, norm_ffn2,
                ffn1_w_up, ffn1_w_down, w_in, sb_q_gain, sb_k_gain, w_out,
                ffn2_w_up, ffn2_w_down):
    B, T = x.shape[0], x.shape[1]
    mod = jax.nn.silu(c) @ w_ada + b_ada
    sh1, sc1, g1, sh2, sc2, g2, sh3, sc3, g3 = jnp.split(mod, N_MOD, axis=-1)

    h = modulate(x, norm_ffn1, sh1, sc1)
    x = x + MACARON_WEIGHT * g1[:, None, :] * swiglu_ffn(h, ffn1_w_up, ffn1_w_down)

    h = modulate(x, norm_mix, sh2, sc2)
    proj = h @ w_in
    splits = [SB_WIDTH, 2 * SB_WIDTH, 3 * SB_WIDTH, 3 * SB_WIDTH + RET_WIDTH,
              3 * SB_WIDTH + 2 * RET_WIDTH, 3 * SB_WIDTH + 3 * RET_WIDTH]
    sb_q, sb_k, sb_v, r_q, r_k, r_v, r_g = jnp.split(proj, splits, axis=-1)
    sb_q = rms_norm(sb_q.reshape(B, T, SB_HEADS, SB_HEAD_DIM), sb_q_gain)
    sb_k = rms_norm(sb_k.reshape(B, T, SB_HEADS, SB_HEAD_DIM), sb_k_gain)
    sb_v = sb_v.reshape(B, T, SB_HEADS, SB_HEAD_DIM)
    r_q = rope(r_q.reshape(B, T, RET_HEADS, RET_HEAD_DIM).astype(jnp.float32), pos)
    r_k = rope(r_k.reshape(B, T, RET_HEADS, RET_HEAD_DIM).astype(jnp.float32), pos) * RET_HEAD_DIM ** -0.5
    r_v = r_v.reshape(B, T, RET_HEADS, RET_HEAD_DIM).astype(jnp.float32)

    if sb_k_past is None:
        sb_o = stick_breaking_blocked(sb_q, sb_k, sb_v, pos)
        ret_s0 = jnp.zeros((B, RET_HEADS, RET_HEAD_DIM, RET_HEAD_DIM), jnp.float32)
    else:
        k_all = jnp.concatenate([sb_k_past.astype(sb_k.dtype), sb_k], axis=1)
        v_all = jnp.concatenate([sb_v_past.astype(sb_v.dtype), sb_v], axis=1)
        k_pos = jnp.arange(k_all.shape[1], dtype=jnp.int32)
        sb_o = stick_breaking(sb_q, k_all, v_all, pos, k_pos)
    ret_o, ret_state = retention(r_q, r_k, r_v, ret_s0)

    mixed = jnp.concatenate(
        [sb_o.reshape(B, T, SB_WIDTH).astype(x.dtype),
         jax.nn.silu(r_g) * head_group_norm(ret_o).reshape(B, T, RET_WIDTH).astype(x.dtype)],
        axis=-1) @ w_out
    x = x + g2[:, None, :] * mixed

    h = modulate(x, norm_ffn2, sh3, sc3)
    x = x + MACARON_WEIGHT * g3[:, None, :] * swiglu_ffn(h, ffn2_w_up, ffn2_w_down)
    return x, sb_k, sb_v, ret_state


def setup_inputs(seed: int = 0) -> dict:
    key = jax.random.key(seed)
    ks = jax.random.split(key, 20)

    def normal(k, shape, scale):
        return jax.random.normal(k, shape, jnp.float32) * scale

    return {
        'x_prompt': normal(ks[0], (BATCH, SEQ, D_MODEL), 1.0),
        'x_sample': normal(ks[1], (DEC_BATCH, DEC_SEQ, D_MODEL), 1.0),
        'cache_sb_k': normal(ks[2], (DEPTH, DEC_BATCH, PAST_LEN, SB_HEADS, SB_HEAD_DIM), 1.0),
        'cache_sb_v': normal(ks[3], (DEPTH, DEC_BATCH, PAST_LEN, SB_HEADS, SB_HEAD_DIM), 1.0),
        'state_ret': normal(ks[4], (DEPTH, DEC_BATCH, RET_HEADS, RET_HEAD_DIM, RET_HEAD_DIM), 0.5),
        'c_prompt': normal(ks[5], (BATCH, D_MODEL), 1.0),
        'c_sample': normal(ks[6], (DEC_BATCH, D_MODEL), 1.0),
        'w_ada': normal(ks[7], (DEPTH, D_MODEL, N_MOD * D_MODEL), 0.5 * D_MODEL ** -0.5),
        'b_ada': normal(ks[8], (DEPTH, N_MOD * D_MODEL), 0.02),
        'norm_ffn1': 1.0 + normal(ks[9], (DEPTH, D_MODEL), 0.02),
        'norm_mix': 1.0 + normal(ks[10], (DEPTH, D_MODEL), 0.02),
        'norm_ffn2': 1.0 + normal(ks[11], (DEPTH, D_MODEL), 0.02),
        'ffn1_w_up': normal(ks[12], (DEPTH, D_MODEL, 2 * D_FF), D_MODEL ** -0.5),
        'ffn1_w_down': normal(ks[13], (DEPTH, D_FF, D_MODEL), D_FF ** -0.5),
        'w_in': normal(ks[14], (DEPTH, D_MODEL, IN_WIDTH), D_MODEL ** -0.5),
        'sb_q_gain': 1.0 + normal(ks[15], (DEPTH, SB_HEAD_DIM), 0.02),
        'sb_k_gain': 1.0 + normal(ks[16], (DEPTH, SB_HEAD_DIM), 0.02),
        'w_out': normal(ks[17], (DEPTH, MIX_WIDTH, D_MODEL), MIX_WIDTH ** -0.5),
        'ffn2_w_up': normal(ks[18], (DEPTH, D_MODEL, 2 * D_FF), D_MODEL ** -0.5),
        'ffn2_w_down': normal(ks[19], (DEPTH, D_FF, D_MODEL), D_FF ** -0.5),
    }


def reference(x_prompt, x_sample, cache_sb_k, cache_sb_v, state_ret, c_prompt, c_sample,
              w_ada, b_ada, norm_ffn1, norm_mix, norm_ffn2, ffn1_w_up, ffn1_w_down,
              w_in, sb_q_gain, sb_k_gain, w_out, ffn2_w_up, ffn2_w_down):
    pos_p = jnp.arange(x_prompt.shape[1], dtype=jnp.int32)
    pos_s = cache_sb_k.shape[2] + jnp.arange(x_sample.shape[1], dtype=jnp.int32)
    y_prompt, y_sample = x_prompt, x_sample
    kp_list, vp_list, sp_list, ks_list, vs_list, ss_list = [], [], [], [], [], []
    for l in range(DEPTH):
        layer_w = (w_ada[l], b_ada[l], norm_ffn1[l], norm_mix[l], norm_ffn2[l],
                   ffn1_w_up[l], ffn1_w_down[l], w_in[l], sb_q_gain[l], sb_k_gain[l],
                   w_out[l], ffn2_w_up[l], ffn2_w_down[l])
        y_prompt, kp, vp, sp = trunk_layer(y_prompt, c_prompt, pos_p, None, None, None, *layer_w)
        y_sample, ksm, vsm, ssm = trunk_layer(y_sample, c_sample, pos_s, cache_sb_k[l],
                                              cache_sb_v[l], state_ret[l], *layer_w)
        kp_list.append(kp)
        vp_list.append(vp)
        sp_list.append(sp)
        ks_list.append(ksm)
        vs_list.append(vsm)
        ss_list.append(ssm)
    new_sb_k_prompt = jnp.stack(kp_list)
    new_sb_v_prompt = jnp.stack(vp_list)
    new_ret_state_prompt = jnp.stack(sp_list)
    new_sb_k_sample = jnp.stack(ks_list)
    new_sb_v_sample = jnp.stack(vs_list)
    new_ret_state_sample = jnp.stack(ss_list)
    return (y_prompt, y_sample, new_sb_k_prompt, new_sb_v_prompt, new_ret_state_prompt,
            new_sb_k_sample, new_sb_v_sample, new_ret_state_sample)
```

```python
import os
import sys
import numpy as np
import ml_dtypes
from contextlib import ExitStack
import concourse.bass as bass
import concourse.mybir as mybir
from concourse.bass_utils import run_bass_kernel_spmd

F32 = mybir.dt.float32
BF16 = mybir.dt.bfloat16
AF = mybir.ActivationFunctionType
ALU = mybir.AluOpType
AX = mybir.AxisListType
NPBF = ml_dtypes.bfloat16

D = 1024
KC = 8
DFF = 2816
HC = 22
NBLK = 32
NSB = 4
NTOK = NBLK * 128 + NSB * 64
PAST = 2048
EPS = 1e-6
RECW = 528
VW = 66
NEG = -30000.0
KSTOP = int(os.environ.get('KSTOP', '9'))
KSUB = int(os.environ.get('KSUB', '99'))
LIMIT = int(os.environ.get('KLIMIT', '1000000000'))


class R:
    __slots__ = ("w", "r", "sem", "cnt", "t", "semname", "psum")

    def __init__(self, t=None, psum=False):
        self.psum = psum
        self.w = {}
        self.r = {}
        self.sem = None
        self.cnt = 0
        self.t = t
        self.semname = None

    def __getitem__(self, k):
        return self.t[k]


class Sched:
    def __init__(self, nc):
        self.nc = nc
        self.E = {}
        self.sems = {}
        for name, e in (("pe", nc.tensor), ("act", nc.scalar), ("dve", nc.vector),
                        ("pool", nc.gpsimd), ("sp", nc.sync)):
            sem = nc.alloc_semaphore("s_" + name)
            self.sems["s_" + name] = sem
            self.E[name] = dict(e=e, sem="s_" + name, n=0, known={}, name=name)
        self.nsem = 5
        self.uid = 0
        self.es = None
        self.dsems = []
        self.nops = 0
        self.log = []

    def sb(self, name, shape, dt):
        if self.es is not None:
            return R(self.es.enter_context(self.nc.sbuf_tensor(name, list(shape), dt)))
        return R(self.nc.alloc_sbuf_tensor(name, list(shape), dt))

    def _deps(self, reads, writes):
        d = {}
        for x in reads:
            for k, v in x.w.items():
                if d.get(k, 0) < v:
                    d[k] = v
            if x.psum:
                for k, v in x.r.items():
                    if d.get(k, 0) < v:
                        d[k] = v
        for x in writes:
            for k, v in x.w.items():
                if d.get(k, 0) < v:
                    d[k] = v
            for k, v in x.r.items():
                if d.get(k, 0) < v:
                    d[k] = v
        return d

    def _wait(self, E, d):
        kn = E["known"]
        for k, v in d.items():
            if E["name"] == "pe" and k == "s_pe":
                continue
            if kn.get(k, 0) >= v:
                continue
            E["e"].wait_ge(self.sems[k], v)
            kn[k] = v

    def op(self, en, fn, reads=(), writes=(), inc=True):
        self.nops += 1
        if self.nops > LIMIT:
            return None
        self.log.append((en, self.nops, sys._getframe(1).f_lineno))
        E = self.E[en]
        self._wait(E, self._deps(reads, writes))
        ins = fn(E["e"])
        if inc:
            E["n"] += 1
            ins.then_inc(self.sems[E["sem"]], 1)
            val = E["n"]
        else:
            val = E["n"] + 1
        k = E["sem"]
        for x in reads:
            if x.r.get(k, 0) < val:
                x.r[k] = val
        for x in writes:
            if x.w.get(k, 0) < val:
                x.w[k] = val
        return ins

    def dma(self, qn, out, in_, reads, writes, semres):
        self.nops += 1
        if self.nops > LIMIT:
            return None
        self.log.append(("dma-" + qn, self.nops, sys._getframe(1).f_lineno))
        E = self.E[qn]
        self._wait(E, self._deps(reads, writes))
        if semres.sem is None:
            semres.sem = {}
        if qn not in semres.sem:
            self.uid += 1
            nm = "d%d" % self.uid
            ent = [nm, self.nc.alloc_semaphore(nm), 0]
            semres.sem[qn] = ent
            self.sems[nm] = ent[1]
            self.nsem += 1
            self.dsems.append(ent)
        ent = semres.sem[qn]
        ent[2] += 16
        E["e"].dma_start(out=out, in_=in_).then_inc(ent[1], 16)
        k, val = ent[0], ent[2]
        for x in reads:
            if x.r.get(k, 0) < val:
                x.r[k] = val
        for x in writes:
            if x.w.get(k, 0) < val:
                x.w[k] = val

    def barrier(self):
        d = {}
        for name in ("pe", "act", "dve", "pool"):
            E = self.E[name]
            if E["n"] > 0:
                d[E["sem"]] = E["n"]
        for ent in self.dsems:
            d[ent[0]] = ent[2]
        for name in ("pe", "act", "dve", "pool", "sp"):
            self._wait(self.E[name], d)

    def finish(self, resources):
        d = {}
        for x in resources:
            for k, v in list(x.w.items()) + list(x.r.items()):
                if d.get(k, 0) < v:
                    d[k] = v
        for name in ("pe", "act", "dve", "pool"):
            E = self.E[name]
            if E["n"] > 0:
                d[E["sem"]] = max(d.get(E["sem"], 0), E["n"])
        self._wait(self.E["sp"], d)


class Stream:
    def __init__(self, S, name, shape, dt, nbuf, plan, look=2):
        self.S = S
        self.bufs = [S.sb("%s%d" % (name, i), shape, dt) for i in range(nbuf)]
        self.plan = plan
        self.issued = 0
        self.pos = 0
        self.look = look

    def _issue(self):
        src, ap = self.plan[self.issued]
        b = self.bufs[self.issued % len(self.bufs)]
        self.S.dma("sp", b.t[:], ap, [src], [b], b)
        self.issued += 1

    def next(self):
        while self.issued < len(self.plan) and self.issued <= self.pos + self.look:
            self._issue()
        b = self.bufs[self.pos % len(self.bufs)]
        self.pos += 1
        return b


def build_nc():
    nc = bass.Bass("TRN2", target_bir_lowering=False)
    S = Sched(nc)

    def din(name, shape, dt=F32):
        return R(nc.dram_tensor(name, list(shape), dt, kind="ExternalInput"))

    def dout(name, shape, dt=F32):
        return R(nc.dram_tensor(name, list(shape), dt, kind="ExternalOutput"))

    def dscr(name, shape, dt):
        return R(nc.dram_tensor(name, list(shape), dt))

    xm = din("xm", [128, KC, NTOK])
    xo = din("xo", [128, KC, NBLK * 128])
    cT = din("cT", [128, KC, 5])
    wada = din("wada", [128, KC, 9 * D])
    bada = din("bada", [128, 72])
    gains = din("gains", [128, 3, KC])
    up_in = [din("up1", [128, 11, 4, KC, 128]), din("up2", [128, 11, 4, KC, 128])]
    dn_in = [din("dn1", [128, KC, HC, 128]), din("dn2", [128, KC, HC, 128])]
    win_in = din("win", [128, 7, KC, 512])
    wout_in = din("wout", [128, KC, KC, 128])
    qkg = din("qkg", [128, 2, 512])
    cosm = din("cosm", [128, NBLK + NSB, 64])
    sinm = din("sinm", [128, NBLK + NSB, 64])
    coso = din("coso", [128, NBLK, 64])
    sino = din("sino", [128, NBLK, 64])
    dqk = din("dqk", [128, 2, 4])
    gtab = din("gtab", [128, 5, 512])
    cab = din("cab", [128, 2])
    rmask = din("rmask", [128, 128], BF16)
    cbf = din("cbf", [128, 5, 128], BF16)
    mbf = din("mbf", [128, 3, 512], BF16)
    id32 = din("id32", [128, 128])
    ck = din("ck", [4, PAST, 512])
    cv = din("cv", [4, PAST, 512])
    sret = din("sret", [128, 4, 512])

    yT = dout("yT", [128, KC, NTOK])
    nk = dout("nk", [NTOK, 512])
    nv = dout("nv", [NTOK, 512])
    sp_out = dout("sp_out", [128, 512])
    ss_out = dout("ss_out", [128, 4, 512])

    upb = [dscr("up1b", [128, 11, 4, KC, 128], BF16), dscr("up2b", [128, 11, 4, KC, 128], BF16)]
    dnb = [dscr("dn1b", [128, KC, HC, 128], BF16), dscr("dn2b", [128, KC, HC, 128], BF16)]
    winb = dscr("winb", [128, 7, KC, 512], BF16)
    woutb = dscr("woutb", [128, KC, KC, 128], BF16)
    x1s = dscr("x1s", [128, KC, NTOK], F32)
    recm = dscr("recm", [NBLK + NSB, 128, 8, RECW], BF16)
    reco = dscr("reco", [NBLK, 128, 2, RECW], BF16)
    suse = dscr("suse", [NBLK, 128, 512], BF16)
    mixTs = dscr("mixTs", [128, KC, NTOK], BF16)

    es = ExitStack()
    with es:
        def const(name, src, shape, dt):
            r = S.sb(name, shape, dt)
            S.dma("sp", r.t[:], src.t.ap(), [src], [r], r)
            return r

        c_bf = const("c_bf", cbf, [128, 5, 128], BF16)
        c_qkg = const("c_qkg", qkg, [128, 2, 512], F32)
        c_dqk = const("c_dqk", dqk, [128, 2, 4], F32)
        c_gtab = const("c_gtab", gtab, [128, 5, 512], F32)
        c_cab = const("c_cab", cab, [128, 2], F32)
        c_gains = const("c_gains", gains, [128, 3, KC], F32)
        TRI, SELM, SELO, IDN, ONES = (c_bf.t[:, i, :] for i in range(5))
        MOD = S.sb("MOD", [128, 72, 5], F32)
        GS = S.sb("GS", [128, 3, KC, 5], F32)
        HG = S.sb("HG", [128, 3, KC, 5], F32)

        pz = [R(nc.alloc_psum_tensor("pz%d" % i, [128, 1024], F32), True) for i in range(2)]
        pb = [R(nc.alloc_psum_tensor("pb%d" % i, [128, 512], F32), True) for i in range(2)]
        banks = [R(pz[0].t[:, 0:512], True), R(pz[0].t[:, 512:1024], True), R(pz[1].t[:, 0:512], True), R(pz[1].t[:, 512:1024], True),
                 pb[0], pb[1]]
        tbanks = [R(nc.alloc_psum_tensor("tb%d" % i, [128, 1024], BF16), True) for i in range(2)]
        bctr = [0, 0, 0]

        def bank():
            b = banks[bctr[0] % 6]
            bctr[0] += 1
            return b

        def bank2():
            b = pb[bctr[2] % 2]
            bctr[2] += 1
            return b

        def tbank():
            b = tbanks[bctr[1] % 2]
            bctr[1] += 1
            return b

        def _fin():
            S.es = None
            S.barrier()
            S.finish([yT, nk, nv, sp_out, ss_out, recm, reco, suse, x1s, mixTs])
            print("semaphores used:", S.nsem, "nops", S.nops, "instr counts:", {k: v["n"] for k, v in S.E.items()})
            nc._oplog = S.log
            return nc

        if KSTOP < 0:
            return _fin()
        for src, dst in ((up_in[0], upb[0]), (dn_in[0], dnb[0]), (win_in, winb),
                         (wout_in, woutb), (up_in[1], upb[1]), (dn_in[1], dnb[1])):
            npart = 8
            for q in range(npart):
                p0, p1 = q * 16, (q + 1) * 16
                S.dma("pool", dst.t.ap()[p0:p1], src.t.ap()[p0:p1], [src], [dst], dst)

        if KSTOP < 1:
            return _fin()
        with ExitStack() as e0:
            S.es = e0
            csb = S.sb("csb", [128, KC, 5], F32)
            S.dma("sp", csb.t[:], cT.t.ap(), [cT], [csb], csb)
            badasb = S.sb("badasb", [128, 72], F32)
            S.dma("sp", badasb.t[:], bada.t.ap(), [bada], [badasb], badasb)
            scs = S.sb("scs", [128, KC, 5], F32)
            S.op("act", lambda e: e.activation(out=scs.t[:], in_=csb.t[:], func=AF.Silu), [csb], [scs])
            wa = [S.sb("wa%d" % i, [128, KC, D], F32) for i in range(2)]
            pm = bank()
            first = True
            for m in range(9):
                wb = wa[m % 2]
                for kc in range(KC):
                    S.dma("sp", wb.t[:, kc, :], wada.t.ap()[:, kc, m * D:(m + 1) * D], [wada], [wb], wb)
                for fc in range(KC):
                    col = (m * KC + fc) * 5
                    for kc in range(KC):
                        S.op("pe", lambda e, fc=fc, kc=kc, col=col, first=first: e.matmul(
                            pm.t[:, col:col + 5], lhsT=wb.t[:, kc, fc * 128:(fc + 1) * 128],
                            rhs=scs.t[:, kc, :], start=first, stop=(kc == KC - 1), skip_group_check=True),
                            [wb, scs], [pm], inc=(kc == KC - 1))
                        first = False
            S.op("dve", lambda e: e.tensor_tensor(
                out=MOD.t[:], in0=pm.t[:, 0:360].rearrange("p (a b) -> p a b", b=5),
                in1=badasb.t[:].unsqueeze(2).to_broadcast([128, 72, 5]), op=ALU.add), [pm, badasb], [MOD])
            for n in range(3):
                sc = MOD.t[:, (3 * n + 1) * KC:(3 * n + 2) * KC, :]
                gt = MOD.t[:, (3 * n + 2) * KC:(3 * n + 3) * KC, :]
                S.op("dve", lambda e, sc=sc, n=n: e.scalar_tensor_tensor(
                    out=GS.t[:, n], in0=sc, scalar=1.0,
                    in1=c_gains.t[:, n, :].unsqueeze(2).to_broadcast([128, KC, 5]),
                    op0=ALU.add, op1=ALU.mult), [MOD, c_gains], [GS])
                S.op("dve", lambda e, gt=gt, n=n: e.tensor_scalar(
                    out=HG.t[:, n], in0=gt, scalar1=(1.0 if n == 1 else 0.5), scalar2=None,
                    op0=ALU.mult), [MOD], [HG])
            S.barrier()
            S.es = None

        def SH(n, kc, s):
            return MOD.t[:, 3 * n * KC + kc, s:s + 1]

        def fm_norm(xt, n, N, segs, hb, sq, tmp, rs):
            S.op("act", lambda e: e.activation(out=sq.t[:, 0:KC, :N], in_=xt.t[:, :, :N], func=AF.Square), [xt], [sq])
            ps = bank()
            for kc in range(KC):
                S.op("pe", lambda e, kc=kc: e.matmul(ps.t[:, :N], lhsT=ONES, rhs=sq.t[:, kc, :N],
                                                    start=(kc == 0), stop=(kc == KC - 1)),
                     [c_bf, sq], [ps], inc=(kc == KC - 1))
            S.op("act", lambda e: e.activation(out=rs.t[:, :N], in_=ps.t[:, :N], func=AF.Sqrt,
                                               bias=EPS, scale=1.0 / D), [ps], [rs])
            S.op("dve", lambda e: e.reciprocal(out=rs.t[:, :N], in_=rs.t[:, :N]), [rs], [rs])
            i = 0
            for kc in range(KC):
                for (a, b, s) in segs:
                    tn = tmp[i % 2]
                    i += 1
                    S.op("dve", lambda e, kc=kc, a=a, b=b, s=s, tn=tn: e.scalar_tensor_tensor(
                        out=tn.t[:, a:b], in0=xt.t[:, kc, a:b], scalar=GS.t[:, n, kc, s:s + 1],
                        in1=rs.t[:, a:b], op0=ALU.mult, op1=ALU.mult), [xt, GS, rs], [tn])
                    S.op("act", lambda e, kc=kc, a=a, b=b, s=s, tn=tn: e.activation(
                        out=hb.t[:, kc, a:b], in_=tn.t[:, a:b], func=AF.Identity, bias=SH(n, kc, s), scale=1.0),
                        [tn, MOD], [hb])

        def ffn(xt, n, N, segs, hb, hid, sgt, ups, dns):
            for s in range(11):
                W = ups.next()
                for j in range(2):
                    pg, pu = bank(), bank()
                    for kc in range(KC):
                        S.op("pe", lambda e, kc=kc, j=j, pg=pg: e.matmul(
                            pg.t[:, :N], lhsT=W.t[:, j, kc, :], rhs=hb.t[:, kc, :N],
                            start=(kc == 0), stop=(kc == KC - 1)), [W, hb], [pg], inc=(kc == KC - 1))
                    for kc in range(KC):
                        S.op("pe", lambda e, kc=kc, j=j, pu=pu: e.matmul(
                            pu.t[:, :N], lhsT=W.t[:, 2 + j, kc, :], rhs=hb.t[:, kc, :N],
                            start=(kc == 0), stop=(kc == KC - 1)), [W, hb], [pu], inc=(kc == KC - 1))
                    sg = sgt[0]
                    S.op("act", lambda e, pg=pg, sg=sg: e.activation(out=sg.t[:, :N], in_=pg.t[:, :N], func=AF.Silu),
                         [pg], [sg])
                    S.op("dve", lambda e, pu=pu, sg=sg, c=2 * s + j: e.tensor_tensor(
                        out=hid.t[:, c, :N], in0=sg.t[:, :N], in1=pu.t[:, :N], op=ALU.mult), [sg, pu], [hid])
            for oc in range(KC):
                W = dns.next()
                ps = bank()
                for kc in range(HC):
                    S.op("pe", lambda e, kc=kc: e.matmul(ps.t[:, :N], lhsT=W.t[:, kc, :], rhs=hid.t[:, kc, :N],
                                                        start=(kc == 0), stop=(kc == HC - 1)),
                         [W, hid], [ps], inc=(kc == HC - 1))
                for (a, b, sq_) in segs:
                    S.op("dve", lambda e, a=a, b=b, sq_=sq_, oc=oc: e.scalar_tensor_tensor(
                        out=xt.t[:, oc, a:b], in0=ps.t[:, a:b], scalar=HG.t[:, n, oc, sq_:sq_ + 1],
                        in1=xt.t[:, oc, a:b], op0=ALU.mult, op1=ALU.add), [ps, HG, xt], [xt])

        PSEG = lambda N: [(0, N, 0)]
        SSEG = [(64 * i, 64 * (i + 1), 1 + i) for i in range(4)]

        if KSTOP < 2:
            return _fin()
        with ExitStack() as e1:
            S.es = e1
            up_plan, dn_plan = [], []
            order = []
            for t in range(8):
                order += [("m", t), ("o", t)]
            order.append(("s", 0))
            for _ in order:
                up_plan += [(upb[0], upb[0].t.ap()[:, s]) for s in range(11)]
                dn_plan += [(dnb[0], dnb[0].t.ap()[:, oc]) for oc in range(KC)]
            ups = Stream(S, "ups", [128, 4, KC, 128], BF16, 2, up_plan, look=1)
            dns = Stream(S, "dns", [128, HC, 128], BF16, 2, dn_plan, look=1)
            winr = S.sb("winr", [128, 7, KC, 512], BF16)
            for sl in range(7):
                S.dma("sp", winr.t[:, sl], winb.t.ap()[:, sl], [winb], [winr], winr)
            xt = S.sb("xt", [128, KC, 512], F32)
            hb = S.sb("hb", [128, KC, 512], BF16)
            tmp = [S.sb("tmpn%d" % i, [128, 512], F32) for i in range(2)]
            rs = S.sb("rs", [128, 512], F32)
            hid = S.sb("hid", [128, HC, 512], BF16)
            sq = hid
            sgt = [S.sb("sgt%d" % i, [128, 512], F32) for i in range(1)]
            cst = [S.sb("cst%d" % i, [128, 2, 64], F32) for i in range(2)]
            rect = [S.sb("rect%d" % i, [128, 8, RECW], BF16) for i in range(2)]
            recto = [S.sb("recto%d" % i, [128, 2, RECW], BF16) for i in range(2)]
            for rr in rect:
                S.op("dve", lambda e, rr=rr: e.memset(rr.t[:], 0.0), [], [rr])
                S.op("dve", lambda e, rr=rr: e.memset(rr.t[:, 7, :], 1.0), [], [rr])
            for rr in recto:
                S.op("dve", lambda e, rr=rr: e.memset(rr.t[:], 0.0), [], [rr])
                S.op("dve", lambda e, rr=rr: e.memset(rr.t[:, 1, :], 1.0), [], [rr])
            w1 = S.sb("w1", [128, 512], F32)
            w2 = S.sb("w2", [128, 512], F32)
            w3, w4 = tmp[0], tmp[1]
            st8 = S.sb("st8", [128, 8], F32)
            o32 = [S.sb("o32_%d" % i, [128, 512], F32) for i in range(2)]
            tokb = [S.sb("tokb%d" % i, [128, 512], BF16) for i in range(2)]
            rkb = S.sb("rkb", [128, 512], BF16)
            rvb = S.sb("rvb", [128, 512], BF16)
            Lm = [S.sb("Lm%d" % i, [128, 512], F32) for i in range(4)]
            Lo = [S.sb("Lo%d" % i, [128, 512], F32) for i in range(1)]
            Sst = S.sb("Sst", [128, 512], F32)
            sc1 = S.sb("sc1", [128, 512], F32)
            sc2 = S.sb("sc2", [128, 512], F32)
            sc3 = S.sb("sc3", [128, 512], F32)
            sub = S.sb("sub", [128, 512], BF16)
            S.op("dve", lambda e: e.memset(Sst.t[:], 0.0), [], [Sst])
            cnt = dict(o32=0, tokb=0, rect=0, recto=0)

            def qknorm(ps, nt, gi, want32):
                S.op("act", lambda e: e.activation(out=w1.t[:nt], in_=ps.t[:nt], func=AF.Square), [ps], [w1])
                S.op("dve", lambda e: e.tensor_reduce(
                    out=st8.t[:nt], in_=w1.t[:nt].rearrange("p (h d) -> p h d", d=64), axis=AX.X, op=ALU.add),
                    [w1], [st8])
                S.op("act", lambda e: e.activation(out=st8.t[:nt], in_=st8.t[:nt], func=AF.Sqrt, bias=EPS,
                                                   scale=1.0 / 64), [st8], [st8])
                S.op("dve", lambda e: e.reciprocal(out=st8.t[:nt], in_=st8.t[:nt]), [st8], [st8])
                S.op("dve", lambda e: e.tensor_tensor(
                    out=w2.t[:nt].rearrange("p (h d) -> p h d", d=64),
                    in0=ps.t[:nt].rearrange("p (h d) -> p h d", d=64),
                    in1=st8.t[:nt].unsqueeze(2).to_broadcast([nt, 8, 64]), op=ALU.mult), [ps, st8], [w2])
                tb = tokb[cnt["tokb"] % 2]
                cnt["tokb"] += 1
                o = None
                if want32:
                    o = o32[cnt["o32"] % 2]
                    cnt["o32"] += 1
                    S.op("dve", lambda e: e.tensor_tensor(out=o.t[:nt], in0=w2.t[:nt], in1=c_qkg.t[:nt, gi, :],
                                                           op=ALU.mult), [w2, c_qkg], [o])
                S.op("dve", lambda e: e.tensor_tensor(out=tb.t[:nt], in0=w2.t[:nt], in1=c_qkg.t[:nt, gi, :],
                                                      op=ALU.mult), [w2, c_qkg], [tb])
                return tb, o

            def transpose4(tb, nt, dst, dstap):
                tp = tbank()
                for c in range(4):
                    S.op("pe", lambda e, c=c: e.transpose(out=tp.t[:, c * 128:c * 128 + nt],
                                                          in_=tb.t[:nt, c * 128:(c + 1) * 128],
                                                          identity=IDN[:nt, :nt]), [tb, c_bf], [tp], inc=(c == 3))
                S.op("act", lambda e: e.activation(
                    out=dstap.rearrange("p (c t) -> p c t", t=128)[:, :, :nt],
                    in_=tp.t[:, 0:512].rearrange("p (c t) -> p c t", t=128)[:, :, :nt], func=AF.Copy), [tp], [dst])

            def rope(ps, nt, cs, di, outb):
                v = ps.t[:nt].rearrange("p (h two d) -> p h two d", two=2, d=64)
                x1, x2 = v[:, :, 0, :], v[:, :, 1, :]
                cb = cs.t[:nt, 0:1, :].to_broadcast([nt, 4, 64])
                sb_ = cs.t[:nt, 1:2, :].to_broadcast([nt, 4, 64])
                a1 = w1.t[:nt, 0:256].rearrange("p (h d) -> p h d", d=64)
                a2 = w2.t[:nt, 0:256].rearrange("p (h d) -> p h d", d=64)
                a3 = w3.t[:nt, 0:256].rearrange("p (h d) -> p h d", d=64)
                a4 = w4.t[:nt, 0:256].rearrange("p (h d) -> p h d", d=64)
                S.op("dve", lambda e: e.tensor_tensor(out=a1, in0=x1, in1=cb, op=ALU.mult), [ps, cs], [w1])
                S.op("dve", lambda e: e.tensor_tensor(out=a2, in0=x2, in1=sb_, op=ALU.mult), [ps, cs], [w2])
                S.op("dve", lambda e: e.tensor_tensor(out=a3, in0=x1, in1=sb_, op=ALU.mult), [ps, cs], [w3])
                S.op("dve", lambda e: e.tensor_tensor(out=a4, in0=x2, in1=cb, op=ALU.mult), [ps, cs], [w4])
                r = w1.t[:nt, 256:512].rearrange("p (h d) -> p h d", d=64)
                r2 = w2.t[:nt, 256:512].rearrange("p (h d) -> p h d", d=64)
                S.op("dve", lambda e: e.tensor_tensor(out=r, in0=a1, in1=a2, op=ALU.subtract), [w1, w2], [w1])
                S.op("dve", lambda e: e.tensor_tensor(out=r2, in0=a3, in1=a4, op=ALU.add), [w3, w4], [w2])
                ov = outb.t[:nt].rearrange("p (h two d) -> p h two d", two=2, d=64)
                dc = c_dqk.t[:nt, di, :].unsqueeze(2).to_broadcast([nt, 4, 64])
                S.op("dve", lambda e: e.tensor_tensor(out=ov[:, :, 0, :], in0=r, in1=dc, op=ALU.mult),
                     [w1, c_dqk], [outb])
                S.op("dve", lambda e: e.tensor_tensor(out=ov[:, :, 1, :], in0=r2, in1=dc, op=ALU.mult),
                     [w2, c_dqk], [outb])

            def proj(col0, nt, sl):
                ps = bank()
                for kc in range(KC):
                    S.op("pe", lambda e, kc=kc: e.matmul(ps.t[:nt, :], lhsT=hb.t[:, kc, col0:col0 + nt],
                                                        rhs=winr.t[:, sl, kc, :], start=(kc == 0),
                                                        stop=(kc == KC - 1)), [hb, winr], [ps], inc=(kc == KC - 1))
                return ps

            def tm_block(kind, bi, col0, nt, tok0, cs):
                mine = kind != "o"
                if mine:
                    rt = rect[cnt["rect"] % 2]
                    cnt["rect"] += 1
                    kslot, vslot = 6, 7
                else:
                    rt = recto[cnt["recto"] % 2]
                    cnt["recto"] += 1
                    kslot, vslot = 0, 1
                if mine:
                    ps = proj(col0, nt, 0)
                    tb, _ = qknorm(ps, nt, 0, False)
                    transpose4(tb, nt, rt, rt.t[:, 0, 0:512])
                ps = proj(col0, nt, 1)
                tb, o = qknorm(ps, nt, 1, mine)
                if mine:
                    S.dma("pool", nk.t.ap()[tok0:tok0 + nt, :], o.t[:nt], [o], [nk], o)
                transpose4(tb, nt, rt, rt.t[:, kslot, 0:512])
                ps = proj(col0, nt, 2)
                if mine:
                    o = o32[cnt["o32"] % 2]
                    cnt["o32"] += 1
                    S.op("act", lambda e: e.activation(out=o.t[:nt], in_=ps.t[:nt], func=AF.Copy), [ps], [o])
                    S.dma("pool", nv.t.ap()[tok0:tok0 + nt, :], o.t[:nt], [o], [nv], o)
                S.op("dve", lambda e: e.tensor_copy(
                    out=rt.t[:nt, vslot, :].rearrange("p (h d) -> p h d", d=VW)[:, :, 0:64],
                    in_=ps.t[:nt].rearrange("p (h d) -> p h d", d=64)), [ps], [rt])
                if mine:
                    ps = proj(col0, nt, 3)
                    tb = tokb[cnt["tokb"] % 2]
                    cnt["tokb"] += 1
                    rope(ps, nt, cs, 0, tb)
                    transpose4(tb, nt, rt, rt.t[:, 1, 0:512])
                ps = proj(col0, nt, 4)
                rope(ps, nt, cs, 1, rkb)
                if mine:
                    transpose4(rkb, nt, rt, rt.t[:, 2, 0:512])
                ps = proj(col0, nt, 5)
                S.op("act", lambda e: e.activation(out=rvb.t[:nt], in_=ps.t[:nt], func=AF.Copy), [ps], [rvb])
                if mine:
                    S.op("dve", lambda e: e.tensor_copy(out=rt.t[:nt, 3, 0:512], in_=rvb.t[:nt]), [rvb], [rt])
                    ps = proj(col0, nt, 6)
                    S.op("act", lambda e: e.activation(out=rt.t[:nt, 4, 0:512], in_=ps.t[:nt], func=AF.Silu),
                         [ps], [rt])
                pl = bank()
                for h in range(4):
                    S.op("pe", lambda e, h=h: e.matmul(pl.t[:, h * 128:(h + 1) * 128],
                                                       lhsT=rkb.t[:nt, h * 128:(h + 1) * 128],
                                                       rhs=rvb.t[:nt, h * 128:(h + 1) * 128],
                                                       start=(h == 0), stop=(h == 3), skip_group_check=True),
                         [rkb, rvb], [pl], inc=(h == 3))
                if kind == "m":
                    S.op("act", lambda e: e.activation(out=Lm[bi % 4].t[:], in_=pl.t[:], func=AF.Copy), [pl], [Lm[bi % 4]])
                    S.dma("pool", recm.t.ap()[bi], rt.t[:], [rt], [recm], rt)
                elif kind == "o":
                    S.op("act", lambda e: e.activation(out=Lo[0].t[:], in_=pl.t[:], func=AF.Copy), [pl], [Lo[0]])
                    S.dma("pool", reco.t.ap()[bi], rt.t[:], [rt], [reco], rt)
                    scan_step(bi)
                else:
                    s = bi - NBLK
                    S.dma("sp", sc3.t[:], sret.t.ap()[:, s, :], [sret], [sc3], sc3)
                    S.op("dve", lambda e: e.tensor_copy(out=rt.t[:, 5, 0:512], in_=sc3.t[:]), [sc3], [rt])
                    S.op("dve", lambda e: e.tensor_tensor(out=sc1.t[:], in0=pl.t[:], in1=sc3.t[:], op=ALU.add),
                         [pl, sc3], [sc1])
                    S.op("dve", lambda e: e.tensor_tensor(out=sc1.t[:], in0=sc1.t[:], in1=c_gtab.t[:, 4, :],
                                                          op=ALU.mult), [sc1, c_gtab], [sc1])
                    S.dma("pool", ss_out.t.ap()[:, s, :], sc1.t[:], [sc1], [ss_out], sc1)
                    S.dma("pool", recm.t.ap()[bi], rt.t[:], [rt], [recm], rt)

            def scan_step(i):
                lm, lo = Lm[i % 4], Lo[0]
                G128, G256, CM, CO = (c_gtab.t[:, k, :] for k in range(4))
                S.op("dve", lambda e: e.tensor_tensor(out=sc1.t[:], in0=Sst.t[:], in1=lo.t[:], op=ALU.add),
                     [Sst, lo], [sc1])
                S.op("dve", lambda e: e.tensor_tensor(out=sc1.t[:], in0=sc1.t[:], in1=G128, op=ALU.mult),
                     [sc1, c_gtab], [sc1])
                S.op("dve", lambda e: e.tensor_scalar(out=sc2.t[:], in0=Sst.t[:], scalar1=c_cab.t[:, 0:1],
                                                      scalar2=None, op0=ALU.mult), [Sst, c_cab], [sc2])
                S.op("dve", lambda e: e.scalar_tensor_tensor(out=sub.t[:], in0=sc1.t[:], scalar=c_cab.t[:, 1:2],
                                                             in1=sc2.t[:], op0=ALU.mult, op1=ALU.add),
                     [sc1, sc2, c_cab], [sub])
                S.dma("pool", suse.t.ap()[i], sub.t[:], [sub], [suse], sub)
                S.op("dve", lambda e: e.tensor_tensor(out=sc2.t[:], in0=lm.t[:], in1=CM, op=ALU.mult),
                     [lm, c_gtab], [sc2])
                S.op("dve", lambda e: e.tensor_tensor(out=sc3.t[:], in0=lo.t[:], in1=CO, op=ALU.mult),
                     [lo, c_gtab], [sc3])
                S.op("dve", lambda e: e.tensor_tensor(out=Sst.t[:], in0=Sst.t[:], in1=G256, op=ALU.mult),
                     [Sst, c_gtab], [Sst])
                S.op("dve", lambda e: e.tensor_tensor(out=sc2.t[:], in0=sc2.t[:], in1=sc3.t[:], op=ALU.add),
                     [sc2, sc3], [sc2])
                S.op("dve", lambda e: e.tensor_tensor(out=Sst.t[:], in0=Sst.t[:], in1=sc2.t[:], op=ALU.add),
                     [Sst, sc2], [Sst])

            for (kind, t) in order:
                N = 256 if kind == "s" else 512
                src = xo if kind == "o" else xm
                c0 = NBLK * 128 if kind == "s" else t * 512
                segs = SSEG if kind == "s" else PSEG(N)
                S.dma("sp", xt.t[:, :, :N], src.t.ap()[:, :, c0:c0 + N], [src], [xt], xt)
                if KSUB < 1:
                    return _fin()
                fm_norm(xt, 0, N, segs, hb, sq, tmp, rs)
                if KSUB < 2:
                    return _fin()
                ffn(xt, 0, N, segs, hb, hid, sgt, ups, dns)
                if KSUB < 3:
                    return _fin()
                if kind != "o" and not os.environ.get("NOX1"):
                    S.dma(os.environ.get("X1Q", "pool"), x1s.t.ap()[:, :, c0:c0 + N], xt.t[:, :, :N], [xt], [x1s], xt)
                fm_norm(xt, 1, N, segs, hb, sq, tmp, rs)
                nb = 4
                nt = 64 if kind == "s" else 128
                for b in range(nb):
                    bi = (NBLK + b) if kind == "s" else (t * 4 + b)
                    cs = cst[(cnt["rect"] + cnt["recto"]) % 2]
                    ctab, stab = (coso, sino) if kind == "o" else (cosm, sinm)
                    S.dma("sp", cs.t[:, 0, :], ctab.t.ap()[:, bi, :], [ctab], [cs], cs)
                    S.dma("sp", cs.t[:, 1, :], stab.t.ap()[:, bi, :], [stab], [cs], cs)
                    if KSUB < 4:
                        return _fin()
                    tm_block(kind, bi, b * nt, nt, c0 + b * nt, cs)
                    if KSUB < 5:
                        return _fin()
            S.dma("pool", sp_out.t.ap(), Sst.t[:], [Sst], [sp_out], Sst)
            S.barrier()
            S.es = None


        if KSTOP < 3:
            return _fin()
        with ExitStack() as e2:
            S.es = e2
            m_bf = const("m_bf", mbf, [128, 3, 512], BF16)
            c_id32 = const("c_id32", id32, [128, 128], F32)
            c_rmask = const("c_rmask", rmask, [128, 128], BF16)
            KTm = S.sb("KTm", [128, NBLK, 512], BF16)
            KTo = S.sb("KTo", [128, NBLK, 512], BF16)
            Vm = S.sb("Vm", [128, NBLK, RECW], BF16)
            Vo = S.sb("Vo", [128, NBLK, RECW], BF16)
            for q in range(4):
                b0, b1 = q * 8, (q + 1) * 8
                S.dma("sp", KTm.t[:, b0:b1, :], recm.t.ap()[b0:b1, :, 6, 0:512].rearrange("b p f -> p b f"),
                      [recm], [KTm], KTm)
                S.dma("sp", KTo.t[:, b0:b1, :], reco.t.ap()[b0:b1, :, 0, 0:512].rearrange("b p f -> p b f"),
                      [reco], [KTo], KTo)
                S.dma("sp", Vm.t[:, b0:b1, :], recm.t.ap()[b0:b1, :, 7, :].rearrange("b p f -> p b f"),
                      [recm], [Vm], Vm)
                S.dma("sp", Vo.t[:, b0:b1, :], reco.t.ap()[b0:b1, :, 1, :].rearrange("b p f -> p b f"),
                      [reco], [Vo], Vo)
            ework = S.sb("ework", [128, 1024], F32)
            spw = [S.sb("spw%d" % i, [128, 1024], BF16) for i in range(2)]
            ww = [S.sb("ww%d" % i, [128, 1024], BF16) for i in range(2)]
            acc = S.sb("acc", [128, 8, 64], F32)
            tmpo = S.sb("tmpo", [128, 8, 64], F32)
            fst = S.sb("fst", [128, 8], F32)
            ft = S.sb("ft", [128, 8], F32)
            mixtok = S.sb("mixtok", [128, 1024], BF16)
            mixT = [S.sb("mixT%d" % i, [128, KC, 128], BF16) for i in range(2)]
            rec2 = [S.sb("rec2_%d" % i, [128, 8, RECW], BF16) for i in range(2)]
            sus = [S.sb("sus%d" % i, [128, 512], BF16) for i in range(2)]
            PT = S.sb("PT", [128, 512], BF16)
            g1 = S.sb("g1", [128, 512], F32)
            g2 = S.sb("g2", [128, 512], F32)
            s4 = S.sb("s4", [128, 4, 4], F32)
            stg = [S.sb("stg%d" % i, [128, 512], F32) for i in range(2)]
            ucnt = [0]
            Qz = [S.sb("Qz%d" % i, [128, 8, 128], BF16) for i in range(2)]
            for qq in Qz:
                S.op("dve", lambda e, qq=qq: e.memset(qq.t[:], 0.0), [], [qq])

            def fill_qz(qz, rt, nq):
                qv = qz.t[:].rearrange("p (c two) t -> p c two t", two=2)
                S.op("dve", lambda e: e.tensor_copy(
                    out=qv[0:64, :, 0, :nq],
                    in_=rt.t[0:64, 0, 0:512].rearrange("p (c t) -> p c t", t=128)[:, :, :nq]), [rt], [qz])
                S.op("dve", lambda e: e.tensor_copy(
                    out=qv[64:128, :, 1, :nq],
                    in_=rt.t[64:128, 0, 0:512].rearrange("p (c t) -> p c t", t=128)[:, :, :nq]), [rt], [qz])

            def uA(U):
                Q, hlist, nq, sides, nkeys, masks = U["Q"], U["hl"], U["nq"], U["sides"], U["nkeys"], U["masks"]
                u = U["idx"]
                zt = pz[u % 2]
                sp_ = spw[u % 2]
                nh = len(hlist)
                W = nh * nq
                ns = len(sides)
                for si, sd in enumerate(sides):
                    for hh, h in enumerate(hlist):
                        c, r0 = h // 2, (h % 2) * 64
                        last = (hh == nh - 1) and masks is None
                        S.op("pe", lambda e, si=si, sd=sd, hh=hh, c=c, r0=r0, h=h: e.matmul(
                            zt.t[:nkeys, si * 512 + hh * nq:si * 512 + (hh + 1) * nq],
                            lhsT=sd["kt"](c, r0), rhs=Q.t[:, h, 0:nq],
                            start=(hh == 0), stop=False, skip_group_check=True),
                            [sd["ktR"], Q], [zt], inc=last)
                    if masks is not None:
                        S.op("pe", lambda e, si=si: e.matmul(
                            zt.t[:nkeys, si * 512:si * 512 + W], lhsT=IDN[:nkeys, :nkeys], rhs=masks[si],
                            start=False, stop=False, skip_group_check=True), [c_bf, m_bf], [zt], inc=True)
                S.op("act", lambda e: e.activation(out=ework.t[:nkeys, 0:ns * 512], in_=zt.t[:nkeys, 0:ns * 512],
                                                   func=AF.Exp, scale=0.125), [zt], [ework])
                S.op("act", lambda e: e.activation(out=sp_.t[:nkeys, 0:ns * 512], in_=ework.t[:nkeys, 0:ns * 512],
                                                   func=AF.Ln, bias=1.0, scale=1.0), [ework], [sp_])

            def uB(U):
                hlist, nq, sides, nkeys, cross = U["hl"], U["nq"], U["sides"], U["nkeys"], U["cross"]
                u = U["idx"]
                zt = pz[u % 2]
                sp_, w_ = spw[u % 2], ww[u % 2]
                W = len(hlist) * nq
                ns = len(sides)
                for si in range(ns):
                    S.op("pe", lambda e, si=si: e.matmul(
                        zt.t[:nkeys, si * 512:si * 512 + W], lhsT=TRI[:nkeys, :nkeys],
                        rhs=sp_.t[:nkeys, si * 512:si * 512 + W], start=False, stop=(not cross),
                        skip_group_check=True), [c_bf, sp_], [zt], inc=(not cross and si == ns - 1))
                if cross:
                    S.op("pe", lambda e: e.matmul(zt.t[:, 0:512], lhsT=SELM, rhs=sp_.t[:, 512:1024],
                                                  start=False, stop=True, skip_group_check=True),
                         [c_bf, sp_], [zt], inc=False)
                    S.op("pe", lambda e: e.matmul(zt.t[:, 512:1024], lhsT=SELO, rhs=sp_.t[:, 0:512],
                                                  start=False, stop=True, skip_group_check=True),
                         [c_bf, sp_], [zt], inc=True)
                S.op("act", lambda e: e.activation(out=w_.t[:nkeys, 0:ns * 512], in_=zt.t[:nkeys, 0:ns * 512],
                                                   func=AF.Exp, scale=0.125), [zt], [w_])

            def uC(U):
                hlist, nq, sides, nkeys = U["hl"], U["nq"], U["sides"], U["nkeys"]
                u = U["idx"]
                w_ = ww[u % 2]
                nh = len(hlist)
                ns = len(sides)
                for g0 in range(0, nh, 4):
                    po = bank2()
                    firstmm = True
                    for hh in range(g0, g0 + 4):
                        h = hlist[hh]
                        for si, sd in enumerate(sides):
                            lastmm = (hh == g0 + 3) and (si == ns - 1)
                            S.op("pe", lambda e, si=si, sd=sd, hh=hh, h=h, firstmm=firstmm, lastmm=lastmm, po=po, g0=g0: e.matmul(
                                po.t[:nq, (hh - g0) * 65:(hh - g0 + 1) * 65],
                                lhsT=w_.t[:nkeys, si * 512 + hh * nq:si * 512 + (hh + 1) * nq], rhs=sd["v"](h),
                                start=firstmm, stop=lastmm, skip_group_check=True),
                                [w_, sd["vR"]], [po], inc=lastmm)
                            firstmm = False
                    pov = po.t[:nq, 0:4 * 65].rearrange("p (h d) -> p h d", d=65)
                    S.op("dve", lambda e, pov=pov, g0=g0: e.tensor_tensor(
                        out=tmpo.t[:nq, g0:g0 + 4, :], in0=pov[:, :, 0:64],
                        in1=fst.t[:nq, g0:g0 + 4].unsqueeze(2).to_broadcast([nq, 4, 64]), op=ALU.mult),
                        [po, fst], [tmpo])
                    S.op("dve", lambda e, g0=g0: e.tensor_tensor(out=acc.t[:nq, g0:g0 + 4, :], in0=acc.t[:nq, g0:g0 + 4, :],
                                                           in1=tmpo.t[:nq, g0:g0 + 4, :], op=ALU.add), [acc, tmpo], [acc])
                    S.op("dve", lambda e, pov=pov, g0=g0: e.tensor_tensor(out=ft.t[:nq, g0:g0 + 4], in0=pov[:, :, 64],
                                                          in1=fst.t[:nq, g0:g0 + 4], op=ALU.mult), [po, fst], [ft])
                    S.op("dve", lambda e, g0=g0: e.tensor_tensor(out=fst.t[:nq, g0:g0 + 4], in0=fst.t[:nq, g0:g0 + 4],
                                                          in1=ft.t[:nq, g0:g0 + 4], op=ALU.subtract), [fst, ft], [fst])

            def reset_acc(nq, nh):
                S.op("dve", lambda e: e.memset(fst.t[:], 1.0), [], [fst])
                S.op("dve", lambda e: e.memset(acc.t[:], 0.0), [], [acc])

            def retention(rt, suR, suap, nt):
                ps = bank2()
                for h in range(4):
                    S.op("pe", lambda e, h=h: e.matmul(ps.t[:nt, h * 128:h * 128 + nt],
                                                       lhsT=rt.t[:, 2, h * 128:h * 128 + nt],
                                                       rhs=rt.t[:, 1, h * 128:h * 128 + nt],
                                                       start=(h == 0), stop=(h == 3), skip_group_check=True),
                         [rt], [ps], inc=(h == 3))
                S.op("dve", lambda e: e.tensor_tensor(
                    out=PT.t[:nt].rearrange("p (h d) -> p h d", d=128)[:, :, :nt],
                    in0=ps.t[:nt].rearrange("p (h d) -> p h d", d=128)[:, :, :nt],
                    in1=c_rmask.t[:nt, :nt].unsqueeze(1).to_broadcast([nt, 4, nt]), op=ALU.mult),
                    [ps, c_rmask], [PT])
                po = bank2()
                for h in range(4):
                    S.op("pe", lambda e, h=h: e.matmul(po.t[:nt, h * 128:(h + 1) * 128],
                                                       lhsT=PT.t[:nt, h * 128:h * 128 + nt],
                                                       rhs=rt.t[:nt, 3, h * 128:(h + 1) * 128],
                                                       start=(h == 0), stop=False, skip_group_check=True),
                         [PT, rt], [po], inc=False)
                    S.op("pe", lambda e, h=h: e.matmul(po.t[:nt, h * 128:(h + 1) * 128],
                                                       lhsT=rt.t[:, 1, h * 128:h * 128 + nt],
                                                       rhs=suap[:, h * 128:(h + 1) * 128],
                                                       start=False, stop=True, skip_group_check=True),
                         [rt, suR], [po], inc=(h == 3))
                pov = po.t[:nt].rearrange("p (h d) -> p h d", d=128)
                S.op("dve", lambda e: e.tensor_reduce(out=s4.t[:nt, 0, :], in_=pov, axis=AX.X, op=ALU.add), [po], [s4])
                S.op("act", lambda e: e.activation(out=g1.t[:nt], in_=po.t[:nt], func=AF.Square), [po], [g1])
                S.op("dve", lambda e: e.tensor_reduce(out=s4.t[:nt, 1, :],
                                                      in_=g1.t[:nt].rearrange("p (h d) -> p h d", d=128),
                                                      axis=AX.X, op=ALU.add), [g1], [s4])
                S.op("dve", lambda e: e.tensor_scalar(out=s4.t[:nt, 0, :], in0=s4.t[:nt, 0, :], scalar1=1.0 / 128,
                                                      scalar2=None, op0=ALU.mult), [s4], [s4])
                S.op("dve", lambda e: e.tensor_tensor(out=s4.t[:nt, 2, :], in0=s4.t[:nt, 0, :], in1=s4.t[:nt, 0, :],
                                                      op=ALU.mult), [s4], [s4])
                S.op("dve", lambda e: e.scalar_tensor_tensor(out=s4.t[:nt, 1, :], in0=s4.t[:nt, 1, :],
                                                             scalar=1.0 / 128, in1=s4.t[:nt, 2, :],
                                                             op0=ALU.mult, op1=ALU.subtract), [s4], [s4])
                S.op("act", lambda e: e.activation(out=s4.t[:nt, 1, :], in_=s4.t[:nt, 1, :], func=AF.Sqrt,
                                                   bias=EPS, scale=1.0), [s4], [s4])
                S.op("dve", lambda e: e.reciprocal(out=s4.t[:nt, 1, :], in_=s4.t[:nt, 1, :]), [s4], [s4])
                S.op("dve", lambda e: e.tensor_tensor(
                    out=g1.t[:nt].rearrange("p (h d) -> p h d", d=128), in0=pov,
                    in1=s4.t[:nt, 0, :].unsqueeze(2).to_broadcast([nt, 4, 128]), op=ALU.subtract), [po, s4], [g1])
                S.op("dve", lambda e: e.tensor_tensor(
                    out=g2.t[:nt].rearrange("p (h d) -> p h d", d=128),
                    in0=g1.t[:nt].rearrange("p (h d) -> p h d", d=128),
                    in1=s4.t[:nt, 1, :].unsqueeze(2).to_broadcast([nt, 4, 128]), op=ALU.mult), [g1, s4], [g2])
                S.op("dve", lambda e: e.tensor_tensor(out=mixtok.t[:nt, 512:1024], in0=g2.t[:nt],
                                                       in1=rt.t[:nt, 4, 0:512], op=ALU.mult), [g2, rt], [mixtok])

            def emit_mix(bi, nt, tok0):
                tp = tbank()
                mt = mixT[bi % 2]
                for c in range(KC):
                    S.op("pe", lambda e, c=c: e.transpose(out=tp.t[:, c * 128:c * 128 + nt],
                                                          in_=mixtok.t[:nt, c * 128:(c + 1) * 128],
                                                          identity=IDN[:nt, :nt]), [mixtok, c_bf], [tp], inc=(c == KC - 1))
                S.op("act", lambda e: e.activation(
                    out=mt.t[:, :, :nt], in_=tp.t[:].rearrange("p (c t) -> p c t", t=128)[:, :, :nt], func=AF.Copy),
                    [tp], [mt])
                S.dma("pool", mixTs.t.ap()[:, :, tok0:tok0 + nt], mt.t[:, :, :nt], [mt], [mixTs], mt)

            units = []

            def mk_prompt_block(j):
                rt = rec2[j % 2]
                su = sus[j % 2]
                qz = Qz[j % 2]

                def preA():
                    S.dma("sp", rt.t[:], recm.t.ap()[j], [recm], [rt], rt)
                    S.dma("sp", su.t[:], suse.t.ap()[j], [suse], [su], su)
                    fill_qz(qz, rt, 128)

                for hg in range(2):
                    hl = [4 * hg + k for k in range(4)]
                    for i in range(j, -1, -1):
                        sides = [
                            dict(ktR=KTm, kt=lambda c, r0, i=i: KTm.t[:, i, c * 128:(c + 1) * 128],
                                 vR=Vm, v=lambda h, i=i: Vm.t[:, i, h * VW:h * VW + 65]),
                            dict(ktR=KTo, kt=lambda c, r0, i=i: KTo.t[:, i, c * 128:(c + 1) * 128],
                                 vR=Vo, v=lambda h, i=i: Vo.t[:, i, h * VW:h * VW + 65]),
                        ]
                        masks = [m_bf.t[:, 0, :], m_bf.t[:, 1, :]] if i == j else None
                        U = dict(Q=qz, hl=hl, nq=128, sides=sides, nkeys=128, masks=masks, cross=True)
                        if hg == 0 and i == j:
                            U["preA"] = preA
                        if i == j:
                            U["preD"] = lambda: reset_acc(128, 4)
                        if i == 0:
                            def postD(hg=hg):
                                S.op("act", lambda e: e.activation(
                                    out=mixtok.t[:, hg * 256:(hg + 1) * 256].rearrange("p (h d) -> p h d", d=64),
                                    in_=acc.t[:, 0:4, :], func=AF.Copy), [acc], [mixtok])
                                if hg == 1:
                                    retention(rt, su, su.t, 128)
                                    emit_mix(j, 128, j * 128)
                            U["postD"] = postD
                        units.append(U)

            def mk_sample_block(s_):
                rt = rec2[s_ % 2]
                qz = Qz[s_ % 2]

                def preA():
                    S.dma("sp", rt.t[:], recm.t.ap()[NBLK + s_], [recm], [rt], rt)
                    for kb in range(16):
                        sk = stg[0]
                        sv = stg[1]
                        S.dma("sp", sk.t[:], ck.t.ap()[s_, kb * 128:(kb + 1) * 128, :], [ck], [sk], sk)
                        S.dma("sp", sv.t[:], cv.t.ap()[s_, kb * 128:(kb + 1) * 128, :], [cv], [sv], sv)
                        tp32 = bank2()
                        for c in range(4):
                            S.op("pe", lambda e, c=c: e.transpose(out=tp32.t[:, c * 128:(c + 1) * 128],
                                                                  in_=sk.t[:, c * 128:(c + 1) * 128],
                                                                  identity=c_id32.t[:]), [sk, c_id32], [tp32],
                                 inc=(c == 3))
                        S.op("act", lambda e, kb=kb: e.activation(out=KTm.t[:, kb, :], in_=tp32.t[:], func=AF.Copy),
                             [tp32], [KTm])
                        S.op("dve", lambda e, kb=kb: e.tensor_copy(
                            out=Vm.t[:, kb, :].rearrange("p (h d) -> p h d", d=VW)[:, :, 0:64],
                            in_=sv.t[:].rearrange("p (h d) -> p h d", d=64)), [sv], [Vm])
                    fill_qz(qz, rt, 64)

                hl = list(range(8))
                sides = [dict(ktR=rt, kt=lambda c, r0: rt.t[:, 6, c * 128:c * 128 + 64],
                              vR=rt, v=lambda h: rt.t[:64, 7, h * VW:h * VW + 65])]
                units.append(dict(Q=qz, hl=hl, nq=64, sides=sides, nkeys=64, masks=[m_bf.t[:64, 2, :]],
                                  cross=False, preA=preA, preD=lambda: reset_acc(64, 8)))
                for kb in range(15, -1, -1):
                    sides = [dict(ktR=KTm, kt=lambda c, r0, kb=kb: KTm.t[:, kb, c * 128:(c + 1) * 128],
                                  vR=Vm, v=lambda h, kb=kb: Vm.t[:, kb, h * VW:h * VW + 65])]
                    U = dict(Q=qz, hl=hl, nq=64, sides=sides, nkeys=128, masks=None, cross=False)
                    if kb == 0:
                        def postD():
                            S.op("act", lambda e: e.activation(
                                out=mixtok.t[:64, 0:512].rearrange("p (h d) -> p h d", d=64),
                                in_=acc.t[:64, 0:8, :], func=AF.Copy), [acc], [mixtok])
                            retention(rt, rt, rt.t[:, 5, 0:512], 64)
                            emit_mix(s_, 64, NBLK * 128 + s_ * 64)
                        U["postD"] = postD
                    units.append(U)

            for j in range(NBLK):
                mk_prompt_block(j)
            npu = len(units)
            for s_ in range(NSB):
                mk_sample_block(s_)
            for k, U in enumerate(units):
                U["idx"] = k

            def run_units(lo, hi):
                def doA(t):
                    if "preA" in units[t]:
                        units[t]["preA"]()
                    uA(units[t])

                def doC(t):
                    U = units[t]
                    if "preD" in U:
                        U["preD"]()
                    uC(U)
                    if "postD" in U:
                        U["postD"]()

                prev = []
                t = lo
                while t < hi:
                    cur = [t] + ([t + 1] if t + 1 < hi else [])
                    for k, u in enumerate(cur):
                        if k < len(prev):
                            doC(prev[k])
                        doA(u)
                    for k in range(len(cur), len(prev)):
                        doC(prev[k])
                    for u in cur:
                        uB(units[u])
                    prev = cur
                    t += 2
                for u in prev:
                    doC(u)

            run_units(0, npu)
            run_units(npu, len(units))
            S.barrier()
            S.es = None

        if KSTOP < 4:
            return _fin()
        with ExitStack() as e3:
            S.es = e3
            up_plan, dn_plan = [], []
            for _ in range(9):
                up_plan += [(upb[1], upb[1].t.ap()[:, s]) for s in range(11)]
                dn_plan += [(dnb[1], dnb[1].t.ap()[:, oc]) for oc in range(KC)]
            ups = Stream(S, "ups2", [128, 4, KC, 128], BF16, 3, up_plan)
            dns = Stream(S, "dns2", [128, HC, 128], BF16, 3, dn_plan)
            woutr = S.sb("woutr", [128, KC, KC, 128], BF16)
            S.dma("sp", woutr.t[:], woutb.t.ap(), [woutb], [woutr], woutr)
            xts = [S.sb("xt2_%d" % i, [128, KC, 512], F32) for i in range(2)]
            mxs = [S.sb("mx%d" % i, [128, KC, 512], BF16) for i in range(2)]
            hb = S.sb("hb2", [128, KC, 512], BF16)
            hid = S.sb("hid2", [128, HC, 512], BF16)
            tmp = [S.sb("tmp2_%d" % i, [128, 512], F32) for i in range(2)]
            rs = S.sb("rs2", [128, 512], F32)
            sgt = [S.sb("sgt2", [128, 512], F32)]
            for t in range(9):
                N = 256 if t == 8 else 512
                c0 = t * 512
                segs = SSEG if t == 8 else PSEG(N)
                xt, mx = xts[t % 2], mxs[t % 2]
                S.dma("sp", xt.t[:, :, :N], x1s.t.ap()[:, :, c0:c0 + N], [x1s], [xt], xt)
                S.dma("sp", mx.t[:, :, :N], mixTs.t.ap()[:, :, c0:c0 + N], [mixTs], [mx], mx)
                for oc in range(KC):
                    ps = bank()
                    for kc in range(KC):
                        S.op("pe", lambda e, kc=kc, oc=oc: e.matmul(ps.t[:, :N], lhsT=woutr.t[:, oc, kc, :],
                                                                    rhs=mx.t[:, kc, :N], start=(kc == 0),
                                                                    stop=(kc == KC - 1)),
                             [woutr, mx], [ps], inc=(kc == KC - 1))
                    for (a, b, sq_) in segs:
                        S.op("dve", lambda e, a=a, b=b, sq_=sq_, oc=oc: e.scalar_tensor_tensor(
                            out=xt.t[:, oc, a:b], in0=ps.t[:, a:b], scalar=HG.t[:, 1, oc, sq_:sq_ + 1],
                            in1=xt.t[:, oc, a:b], op0=ALU.mult, op1=ALU.add), [ps, HG, xt], [xt])
                fm_norm(xt, 2, N, segs, hb, hid, tmp, rs)
                ffn(xt, 2, N, segs, hb, hid, sgt, ups, dns)
                S.dma("pool", yT.t.ap()[:, :, c0:c0 + N], xt.t[:, :, :N], [xt], [yT], xt)
            S.barrier()
            S.es = None

        S.finish([yT, nk, nv, sp_out, ss_out, recm, reco, suse, x1s, mixTs])
    print("semaphores used:", S.nsem, "nops", S.nops, "instr counts:", {k: v["n"] for k, v in S.E.items()})
    nc._oplog = S.log
    return nc


def _fm(x):
    T = x.shape[0]
    return np.ascontiguousarray(x.reshape(T, KC, 128).transpose(2, 1, 0))


def _consts():
    lg = np.log1p(-np.exp2(-5.0 - np.arange(4, dtype=np.float32))).astype(np.float32)
    idx = np.arange(128, dtype=np.float32)
    dq = np.exp((idx + 1.0)[:, None] * lg[None, :]).astype(np.float32)
    dk = (np.exp(-(idx + 1.0)[:, None] * lg[None, :]) * np.float32(128 ** -0.5)).astype(np.float32)
    dqk = np.stack([dq, dk], 1).astype(np.float32)
    g = lambda n: np.repeat(np.exp(np.float32(n) * lg).astype(np.float32), 128)[None, :].repeat(128, 0)
    half = 32
    inv_freq = (10000.0 ** (-np.arange(64, dtype=np.float32) / 64)).astype(np.float32)
    return lg, dqk, g, inv_freq


def _rope_tab(pos, inv_freq):
    ang = pos.astype(np.float32)[:, None] * inv_freq[None, :]
    return np.cos(ang).astype(np.float32), np.sin(ang).astype(np.float32)


_NC_CACHE = {}


def kernel(x_prompt, x_sample, cache_sb_k, cache_sb_v, state_ret, c_prompt, c_sample,
           w_ada, b_ada, norm_ffn1, norm_mix, norm_ffn2, ffn1_w_up, ffn1_w_down,
           w_in, sb_q_gain, sb_k_gain, w_out, ffn2_w_up, ffn2_w_down):
    f = lambda a: np.asarray(a, dtype=np.float32)
    x_prompt, x_sample = f(x_prompt), f(x_sample)
    lg, dqk, g, inv_freq = _consts()

    def kmaj(w, cols):
        return np.ascontiguousarray(w.reshape(KC, 128, cols).transpose(1, 0, 2))

    def up_l(w):
        a = w.reshape(KC, 128, 2, 11, 2, 128)
        a = a.transpose(1, 3, 2, 4, 0, 5)
        return np.ascontiguousarray(a.reshape(128, 11, 4, KC, 128))

    def dn_l(w):
        a = w.reshape(HC, 128, KC, 128).transpose(1, 2, 0, 3)
        return np.ascontiguousarray(a)

    shared = {
        "wada": kmaj(f(w_ada)[0], 9 * D),
        "bada": np.ascontiguousarray(f(b_ada)[0].reshape(72, 128).T),
        "gains": np.ascontiguousarray(np.stack([f(norm_ffn1)[0], f(norm_mix)[0], f(norm_ffn2)[0]], 0)
                                      .reshape(3, KC, 128).transpose(2, 0, 1)),
        "up1": up_l(f(ffn1_w_up)[0]), "up2": up_l(f(ffn2_w_up)[0]),
        "dn1": dn_l(f(ffn1_w_down)[0]), "dn2": dn_l(f(ffn2_w_down)[0]),
        "win": np.ascontiguousarray(f(w_in)[0].reshape(KC, 128, 7, 512).transpose(1, 2, 0, 3)),
        "wout": np.ascontiguousarray(f(w_out)[0].reshape(KC, 128, KC, 128).transpose(1, 2, 0, 3)),
        "qkg": np.ascontiguousarray(np.stack([np.tile(f(sb_q_gain)[0], 8), np.tile(f(sb_k_gain)[0], 8)], 0)[None]
                                    .repeat(128, 0)),
        "dqk": dqk,
        "id32": np.eye(128, dtype=np.float32),
    }
    kk = np.arange(128)
    rmask = (kk[None, :] >= kk[:, None]).astype(np.float32)
    shared["rmask"] = rmask.astype(NPBF)
    tri = np.where(kk[:, None] >= kk[None, :], -8.0, 0.0).astype(np.float32)
    diag = np.where(kk[:, None] < kk[None, :], 0.0, NEG).astype(np.float32)
    ones = np.ones((128, 128), np.float32)
    smp = np.full((128, 64), NEG, np.float32)
    smp[:64] = diag[:64, :64]

    in_maps = []
    for c in range(8):
        s, r = c // 2, c % 2
        xs = x_prompt[s].reshape(64, 128, D)
        mine = xs[r::2].reshape(NBLK * 128, D)
        oth = xs[1 - r::2].reshape(NBLK * 128, D)
        smp_x = x_sample[4 * c:4 * c + 4].reshape(256, D)
        cA, cB = (1.0, 0.0) if r == 0 else (0.0, 1.0)
        posm = (np.arange(NBLK)[:, None] * 2 + r) * 128 + np.arange(128)[None, :]
        poso = (np.arange(NBLK)[:, None] * 2 + (1 - r)) * 128 + np.arange(128)[None, :]
        cm, sm = _rope_tab(posm.reshape(-1), inv_freq)
        co, so = _rope_tab(poso.reshape(-1), inv_freq)
        cs_, ss_ = _rope_tab(PAST + np.arange(64), inv_freq)
        cosm = np.zeros((128, NBLK + NSB, 64), np.float32)
        sinm = np.zeros((128, NBLK + NSB, 64), np.float32)
        cosm[:, :NBLK] = cm.reshape(NBLK, 128, 64).transpose(1, 0, 2)
        sinm[:, :NBLK] = sm.reshape(NBLK, 128, 64).transpose(1, 0, 2)
        cosm[:64, NBLK:] = cs_[:, None, :]
        sinm[:64, NBLK:] = ss_[:, None, :]
        gtab = np.stack([g(128), g(256), cA * g(256) + cB * g(128), cB * g(256) + cA * g(128), g(64)], 1)
        cbf = np.stack([tri, -8.0 * cA * ones, -8.0 * cB * ones, np.eye(128, dtype=np.float32), ones], 1)
        moth = np.full((128, 128), NEG if r == 0 else 0.0, np.float32)
        mbf = np.stack([np.tile(diag, (1, 4)), np.tile(moth, (1, 4)), np.tile(smp, (1, 8))], 1)
        m = dict(shared)
        m.update({
            "xm": np.ascontiguousarray(np.concatenate([_fm(mine), _fm(smp_x)], 2)),
            "xo": _fm(oth),
            "cT": np.ascontiguousarray(np.concatenate([f(c_prompt)[s:s + 1], f(c_sample)[4 * c:4 * c + 4]], 0)
                                       .reshape(5, KC, 128).transpose(2, 1, 0)),
            "cosm": cosm, "sinm": sinm,
            "coso": np.ascontiguousarray(co.reshape(NBLK, 128, 64).transpose(1, 0, 2)),
            "sino": np.ascontiguousarray(so.reshape(NBLK, 128, 64).transpose(1, 0, 2)),
            "gtab": np.ascontiguousarray(gtab.astype(np.float32)),
            "cab": np.tile(np.array([[cA, cB]], np.float32), (128, 1)),
            "cbf": np.ascontiguousarray(cbf).astype(NPBF),
            "mbf": np.ascontiguousarray(mbf).astype(NPBF),
            "ck": np.ascontiguousarray(f(cache_sb_k)[0, 4 * c:4 * c + 4].reshape(4, PAST, 512)),
            "cv": np.ascontiguousarray(f(cache_sb_v)[0, 4 * c:4 * c + 4].reshape(4, PAST, 512)),
            "sret": np.ascontiguousarray(f(state_ret)[0, 4 * c:4 * c + 4].transpose(2, 0, 1, 3).reshape(128, 4, 512)),
        })
        in_maps.append(m)

    if "nc" not in _NC_CACHE:
        _NC_CACHE["nc"] = build_nc()
    res = run_bass_kernel_spmd(_NC_CACHE["nc"], in_maps, core_ids=list(range(8)))

    y_p = np.zeros((4, 64, 128, D), np.float32)
    k_p = np.zeros((4, 64, 128, 512), np.float32)
    v_p = np.zeros((4, 64, 128, 512), np.float32)
    y_s = np.zeros((32, 64, D), np.float32)
    k_s = np.zeros((32, 64, 512), np.float32)
    v_s = np.zeros((32, 64, 512), np.float32)
    st_p = np.zeros((4, 4, 128, 128), np.float32)
    st_s = np.zeros((32, 4, 128, 128), np.float32)
    for c in range(8):
        s, r = c // 2, c % 2
        o = res.results[c]
        yt = np.asarray(o["yT"]).transpose(2, 1, 0).reshape(NTOK, D)
        y_p[s, r::2] = yt[:NBLK * 128].reshape(NBLK, 128, D)
        y_s[4 * c:4 * c + 4] = yt[NBLK * 128:].reshape(4, 64, D)
        nk_ = np.asarray(o["nk"])
        nv_ = np.asarray(o["nv"])
        k_p[s, r::2] = nk_[:NBLK * 128].reshape(NBLK, 128, 512)
        v_p[s, r::2] = nv_[:NBLK * 128].reshape(NBLK, 128, 512)
        k_s[4 * c:4 * c + 4] = nk_[NBLK * 128:].reshape(4, 64, 512)
        v_s[4 * c:4 * c + 4] = nv_[NBLK * 128:].reshape(4, 64, 512)
        if r == 0:
            st_p[s] = np.asarray(o["sp_out"]).reshape(128, 4, 128).transpose(1, 0, 2)
        st_s[4 * c:4 * c + 4] = np.asarray(o["ss_out"]).reshape(128, 4, 4, 128).transpose(1, 2, 0, 3)
    return (y_p.reshape(4, 8192, D), y_s,
            k_p.reshape(1, 4, 8192, 8, 64), v_p.reshape(1, 4, 8192, 8, 64), st_p.reshape(1, 4, 4, 128, 128),
            k_s.reshape(1, 32, 64, 8, 64), v_s.reshape(1, 32, 64, 8, 64), st_s.reshape(1, 32, 4, 128, 128))
```

```python
import os
import sys
import numpy as np
import ml_dtypes
from contextlib import ExitStack
import concourse.bass as bass
import concourse.mybir as mybir
from concourse.bass_utils import run_bass_kernel_spmd

F32 = mybir.dt.float32
BF16 = mybir.dt.bfloat16
AF = mybir.ActivationFunctionType
ALU = mybir.AluOpType
AX = mybir.AxisListType
NPBF = ml_dtypes.bfloat16

D = 1024
KC = 8
DFF = 2816
HC = 22
NBLK = 32
NSB = 4
NTOK = NBLK * 128 + NSB * 64
PAST = 2048
EPS = 1e-6
RECW = 528
VW = 66
NEG = -30000.0
KSTOP = int(os.environ.get('KSTOP', '9'))
KSUB = int(os.environ.get('KSUB', '99'))
LIMIT = int(os.environ.get('KLIMIT', '1000000000'))


class R:
    __slots__ = ("w", "r", "sem", "cnt", "t", "semname", "psum")

    def __init__(self, t=None, psum=False):
        self.psum = psum
        self.w = {}
        self.r = {}
        self.sem = None
        self.cnt = 0
        self.t = t
        self.semname = None

    def __getitem__(self, k):
        return self.t[k]


class Sched:
    def __init__(self, nc):
        self.nc = nc
        self.E = {}
        self.sems = {}
        for name, e in (("pe", nc.tensor), ("act", nc.scalar), ("dve", nc.vector),
                        ("pool", nc.gpsimd), ("sp", nc.sync)):
            sem = nc.alloc_semaphore("s_" + name)
            self.sems["s_" + name] = sem
            self.E[name] = dict(e=e, sem="s_" + name, n=0, known={}, name=name)
        self.nsem = 5
        self.uid = 0
        self.es = None
        self.dsems = []
        self.nops = 0
        self.log = []

    def sb(self, name, shape, dt):
        if self.es is not None:
            return R(self.es.enter_context(self.nc.sbuf_tensor(name, list(shape), dt)))
        return R(self.nc.alloc_sbuf_tensor(name, list(shape), dt))

    def _deps(self, reads, writes):
        d = {}
        for x in reads:
            for k, v in x.w.items():
                if d.get(k, 0) < v:
                    d[k] = v
            if x.psum:
                for k, v in x.r.items():
                    if d.get(k, 0) < v:
                        d[k] = v
        for x in writes:
            for k, v in x.w.items():
                if d.get(k, 0) < v:
                    d[k] = v
            for k, v in x.r.items():
                if d.get(k, 0) < v:
                    d[k] = v
        return d

    def _wait(self, E, d):
        kn = E["known"]
        for k, v in d.items():
            if E["name"] == "pe" and k == "s_pe":
                continue
            if kn.get(k, 0) >= v:
                continue
            E["e"].wait_ge(self.sems[k], v)
            kn[k] = v

    def op(self, en, fn, reads=(), writes=(), inc=True):
        self.nops += 1
        if self.nops > LIMIT:
            return None
        self.log.append((en, self.nops, sys._getframe(1).f_lineno))
        E = self.E[en]
        self._wait(E, self._deps(reads, writes))
        ins = fn(E["e"])
        if inc:
            E["n"] += 1
            ins.then_inc(self.sems[E["sem"]], 1)
            val = E["n"]
        else:
            val = E["n"] + 1
        k = E["sem"]
        for x in reads:
            if x.r.get(k, 0) < val:
                x.r[k] = val
        for x in writes:
            if x.w.get(k, 0) < val:
                x.w[k] = val
        return ins

    def dma(self, qn, out, in_, reads, writes, semres):
        self.nops += 1
        if self.nops > LIMIT:
            return None
        self.log.append(("dma-" + qn, self.nops, sys._getframe(1).f_lineno))
        E = self.E[qn]
        self._wait(E, self._deps(reads, writes))
        if semres.sem is None:
            semres.sem = {}
        if qn not in semres.sem:
            self.uid += 1
            nm = "d%d" % self.uid
            ent = [nm, self.nc.alloc_semaphore(nm), 0]
            semres.sem[qn] = ent
            self.sems[nm] = ent[1]
            self.nsem += 1
            self.dsems.append(ent)
        ent = semres.sem[qn]
        ent[2] += 16
        E["e"].dma_start(out=out, in_=in_).then_inc(ent[1], 16)
        k, val = ent[0], ent[2]
        for x in reads:
            if x.r.get(k, 0) < val:
                x.r[k] = val
        for x in writes:
            if x.w.get(k, 0) < val:
                x.w[k] = val

    def barrier(self):
        d = {}
        for name in ("pe", "act", "dve", "pool"):
            E = self.E[name]
            if E["n"] > 0:
                d[E["sem"]] = E["n"]
        for ent in self.dsems:
            d[ent[0]] = ent[2]
        for name in ("pe", "act", "dve", "pool", "sp"):
            self._wait(self.E[name], d)

    def finish(self, resources):
        d = {}
        for x in resources:
            for k, v in list(x.w.items()) + list(x.r.items()):
                if d.get(k, 0) < v:
                    d[k] = v
        for name in ("pe", "act", "dve", "pool"):
            E = self.E[name]
            if E["n"] > 0:
                d[E["sem"]] = max(d.get(E["sem"], 0), E["n"])
        self._wait(self.E["sp"], d)


class Stream:
    def __init__(self, S, name, shape, dt, nbuf, plan, look=2):
        self.S = S
        self.bufs = [S.sb("%s%d" % (name, i), shape, dt) for i in range(nbuf)]
        self.plan = plan
        self.issued = 0
        self.pos = 0
        self.look = look

    def _issue(self):
        src, ap = self.plan[self.issued]
        b = self.bufs[self.issued % len(self.bufs)]
        self.S.dma("sp", b.t[:], ap, [src], [b], b)
        self.issued += 1

    def next(self):
        while self.issued < len(self.plan) and self.issued <= self.pos + self.look:
            self._issue()
        b = self.bufs[self.pos % len(self.bufs)]
        self.pos += 1
        return b


def build_nc():
    nc = bass.Bass("TRN2", target_bir_lowering=False)
    S = Sched(nc)

    def din(name, shape, dt=F32):
        return R(nc.dram_tensor(name, list(shape), dt, kind="ExternalInput"))

    def dout(name, shape, dt=F32):
        return R(nc.dram_tensor(name, list(shape), dt, kind="ExternalOutput"))

    def dscr(name, shape, dt):
        return R(nc.dram_tensor(name, list(shape), dt))

    xm = din("xm", [128, KC, NTOK])
    xo = din("xo", [128, KC, NBLK * 128])
    cT = din("cT", [128, KC, 5])
    wada = din("wada", [128, KC, 9 * D])
    bada = din("bada", [128, 72])
    gains = din("gains", [128, 3, KC])
    up_in = [din("up1", [128, 11, 4, KC, 128]), din("up2", [128, 11, 4, KC, 128])]
    dn_in = [din("dn1", [128, KC, HC, 128]), din("dn2", [128, KC, HC, 128])]
    win_in = din("win", [128, 7, KC, 512])
    wout_in = din("wout", [128, KC, KC, 128])
    qkg = din("qkg", [128, 2, 512])
    cosm = din("cosm", [128, NBLK + NSB, 64])
    sinm = din("sinm", [128, NBLK + NSB, 64])
    coso = din("coso", [128, NBLK, 64])
    sino = din("sino", [128, NBLK, 64])
    dqk = din("dqk", [128, 2, 4])
    gtab = din("gtab", [128, 5, 512])
    cab = din("cab", [128, 2])
    rmask = din("rmask", [128, 128], BF16)
    cbf = din("cbf", [128, 5, 128], BF16)
    mbf = din("mbf", [128, 3, 512], BF16)
    id32 = din("id32", [128, 128])
    ck = din("ck", [4, PAST, 512])
    cv = din("cv", [4, PAST, 512])
    sret = din("sret", [128, 4, 512])

    yT = dout("yT", [128, KC, NTOK])
    nk = dout("nk", [NTOK, 512])
    nv = dout("nv", [NTOK, 512])
    sp_out = dout("sp_out", [128, 512])
    ss_out = dout("ss_out", [128, 4, 512])

    upb = [dscr("up1b", [128, 11, 4, KC, 128], BF16), dscr("up2b", [128, 11, 4, KC, 128], BF16)]
    dnb = [dscr("dn1b", [128, KC, HC, 128], BF16), dscr("dn2b", [128, KC, HC, 128], BF16)]
    winb = dscr("winb", [128, 7, KC, 512], BF16)
    woutb = dscr("woutb", [128, KC, KC, 128], BF16)
    x1s = dscr("x1s", [128, KC, NTOK], F32)
    recm = dscr("recm", [NBLK + NSB, 128, 8, RECW], BF16)
    reco = dscr("reco", [NBLK, 128, 2, RECW], BF16)
    suse = dscr("suse", [NBLK, 128, 512], BF16)
    mixTs = dscr("mixTs", [128, KC, NTOK], BF16)

    es = ExitStack()
    with es:
        def const(name, src, shape, dt):
            r = S.sb(name, shape, dt)
            S.dma("sp", r.t[:], src.t.ap(), [src], [r], r)
            return r

        c_bf = const("c_bf", cbf, [128, 5, 128], BF16)
        c_qkg = const("c_qkg", qkg, [128, 2, 512], F32)
        c_dqk = const("c_dqk", dqk, [128, 2, 4], F32)
        c_gtab = const("c_gtab", gtab, [128, 5, 512], F32)
        c_cab = const("c_cab", cab, [128, 2], F32)
        c_gains = const("c_gains", gains, [128, 3, KC], F32)
        TRI, SELM, SELO, IDN, ONES = (c_bf.t[:, i, :] for i in range(5))
        MOD = S.sb("MOD", [128, 72, 5], F32)
        GS = S.sb("GS", [128, 3, KC, 5], F32)
        HG = S.sb("HG", [128, 3, KC, 5], F32)

        pz = [R(nc.alloc_psum_tensor("pz%d" % i, [128, 1024], F32), True) for i in range(2)]
        pb = [R(nc.alloc_psum_tensor("pb%d" % i, [128, 512], F32), True) for i in range(2)]
        banks = [R(pz[0].t[:, 0:512], True), R(pz[0].t[:, 512:1024], True), R(pz[1].t[:, 0:512], True), R(pz[1].t[:, 512:1024], True),
                 pb[0], pb[1]]
        tbanks = [R(nc.alloc_psum_tensor("tb%d" % i, [128, 1024], BF16), True) for i in range(2)]
        bctr = [0, 0, 0]

        def bank():
            b = banks[bctr[0] % 6]
            bctr[0] += 1
            return b

        def bank2():
            b = pb[bctr[2] % 2]
            bctr[2] += 1
            return b

        def tbank():
            b = tbanks[bctr[1] % 2]
            bctr[1] += 1
            return b

        def _fin():
            S.es = None
            S.barrier()
            S.finish([yT, nk, nv, sp_out, ss_out, recm, reco, suse, x1s, mixTs])
            print("semaphores used:", S.nsem, "nops", S.nops, "instr counts:", {k: v["n"] for k, v in S.E.items()})
            nc._oplog = S.log
            return nc

        if KSTOP < 0:
            return _fin()
        for src, dst in ((up_in[0], upb[0]), (dn_in[0], dnb[0]), (win_in, winb),
                         (wout_in, woutb), (up_in[1], upb[1]), (dn_in[1], dnb[1])):
            npart = 8
            for q in range(npart):
                p0, p1 = q * 16, (q + 1) * 16
                S.dma("pool", dst.t.ap()[p0:p1], src.t.ap()[p0:p1], [src], [dst], dst)

        if KSTOP < 1:
            return _fin()
        with ExitStack() as e0:
            S.es = e0
            csb = S.sb("csb", [128, KC, 5], F32)
            S.dma("sp", csb.t[:], cT.t.ap(), [cT], [csb], csb)
            badasb = S.sb("badasb", [128, 72], F32)
            S.dma("sp", badasb.t[:], bada.t.ap(), [bada], [badasb], badasb)
            scs = S.sb("scs", [128, KC, 5], F32)
            S.op("act", lambda e: e.activation(out=scs.t[:], in_=csb.t[:], func=AF.Silu), [csb], [scs])
            wa = [S.sb("wa%d" % i, [128, KC, D], F32) for i in range(2)]
            pm = bank()
            first = True
            for m in range(9):
                wb = wa[m % 2]
                for kc in range(KC):
                    S.dma("sp", wb.t[:, kc, :], wada.t.ap()[:, kc, m * D:(m + 1) * D], [wada], [wb], wb)
                for fc in range(KC):
                    col = (m * KC + fc) * 5
                    for kc in range(KC):
                        S.op("pe", lambda e, fc=fc, kc=kc, col=col, first=first: e.matmul(
                            pm.t[:, col:col + 5], lhsT=wb.t[:, kc, fc * 128:(fc + 1) * 128],
                            rhs=scs.t[:, kc, :], start=first, stop=(kc == KC - 1), skip_group_check=True),
                            [wb, scs], [pm], inc=(kc == KC - 1))
                        first = False
            S.op("dve", lambda e: e.tensor_tensor(
                out=MOD.t[:], in0=pm.t[:, 0:360].rearrange("p (a b) -> p a b", b=5),
                in1=badasb.t[:].unsqueeze(2).to_broadcast([128, 72, 5]), op=ALU.add), [pm, badasb], [MOD])
            for n in range(3):
                sc = MOD.t[:, (3 * n + 1) * KC:(3 * n + 2) * KC, :]
                gt = MOD.t[:, (3 * n + 2) * KC:(3 * n + 3) * KC, :]
                S.op("dve", lambda e, sc=sc, n=n: e.scalar_tensor_tensor(
                    out=GS.t[:, n], in0=sc, scalar=1.0,
                    in1=c_gains.t[:, n, :].unsqueeze(2).to_broadcast([128, KC, 5]),
                    op0=ALU.add, op1=ALU.mult), [MOD, c_gains], [GS])
                S.op("dve", lambda e, gt=gt, n=n: e.tensor_scalar(
                    out=HG.t[:, n], in0=gt, scalar1=(1.0 if n == 1 else 0.5), scalar2=None,
                    op0=ALU.mult), [MOD], [HG])
            S.barrier()
            S.es = None

        def SH(n, kc, s):
            return MOD.t[:, 3 * n * KC + kc, s:s + 1]

        def fm_norm(xt, n, N, segs, hb, sq, tmp, rs):
            S.op("act", lambda e: e.activation(out=sq.t[:, 0:KC, :N], in_=xt.t[:, :, :N], func=AF.Square), [xt], [sq])
            ps = bank()
            for kc in range(KC):
                S.op("pe", lambda e, kc=kc: e.matmul(ps.t[:, :N], lhsT=ONES, rhs=sq.t[:, kc, :N],
                                                    start=(kc == 0), stop=(kc == KC - 1)),
                     [c_bf, sq], [ps], inc=(kc == KC - 1))
            S.op("act", lambda e: e.activation(out=rs.t[:, :N], in_=ps.t[:, :N], func=AF.Sqrt,
                                               bias=EPS, scale=1.0 / D), [ps], [rs])
            S.op("dve", lambda e: e.reciprocal(out=rs.t[:, :N], in_=rs.t[:, :N]), [rs], [rs])
            i = 0
            for kc in range(KC):
                for (a, b, s) in segs:
                    tn = tmp[i % 2]
                    i += 1
                    S.op("dve", lambda e, kc=kc, a=a, b=b, s=s, tn=tn: e.scalar_tensor_tensor(
                        out=tn.t[:, a:b], in0=xt.t[:, kc, a:b], scalar=GS.t[:, n, kc, s:s + 1],
                        in1=rs.t[:, a:b], op0=ALU.mult, op1=ALU.mult), [xt, GS, rs], [tn])
                    S.op("act", lambda e, kc=kc, a=a, b=b, s=s, tn=tn: e.activation(
                        out=hb.t[:, kc, a:b], in_=tn.t[:, a:b], func=AF.Identity, bias=SH(n, kc, s), scale=1.0),
                        [tn, MOD], [hb])

        def ffn(xt, n, N, segs, hb, hid, sgt, ups, dns):
            for s in range(11):
                W = ups.next()
                for j in range(2):
                    pg, pu = bank(), bank()
                    for kc in range(KC):
                        S.op("pe", lambda e, kc=kc, j=j, pg=pg: e.matmul(
                            pg.t[:, :N], lhsT=W.t[:, j, kc, :], rhs=hb.t[:, kc, :N],
                            start=(kc == 0), stop=(kc == KC - 1)), [W, hb], [pg], inc=(kc == KC - 1))
                    for kc in range(KC):
                        S.op("pe", lambda e, kc=kc, j=j, pu=pu: e.matmul(
                            pu.t[:, :N], lhsT=W.t[:, 2 + j, kc, :], rhs=hb.t[:, kc, :N],
                            start=(kc == 0), stop=(kc == KC - 1)), [W, hb], [pu], inc=(kc == KC - 1))
                    sg = sgt[0]
                    S.op("act", lambda e, pg=pg, sg=sg: e.activation(out=sg.t[:, :N], in_=pg.t[:, :N], func=AF.Silu),
                         [pg], [sg])
                    S.op("dve", lambda e, pu=pu, sg=sg, c=2 * s + j: e.tensor_tensor(
                        out=hid.t[:, c, :N], in0=sg.t[:, :N], in1=pu.t[:, :N], op=ALU.mult), [sg, pu], [hid])
            for oc in range(KC):
                W = dns.next()
                ps = bank()
                for kc in range(HC):
                    S.op("pe", lambda e, kc=kc: e.matmul(ps.t[:, :N], lhsT=W.t[:, kc, :], rhs=hid.t[:, kc, :N],
                                                        start=(kc == 0), stop=(kc == HC - 1)),
                         [W, hid], [ps], inc=(kc == HC - 1))
                for (a, b, sq_) in segs:
                    S.op("dve", lambda e, a=a, b=b, sq_=sq_, oc=oc: e.scalar_tensor_tensor(
                        out=xt.t[:, oc, a:b], in0=ps.t[:, a:b], scalar=HG.t[:, n, oc, sq_:sq_ + 1],
                        in1=xt.t[:, oc, a:b], op0=ALU.mult, op1=ALU.add), [ps, HG, xt], [xt])

        PSEG = lambda N: [(0, N, 0)]
        SSEG = [(64 * i, 64 * (i + 1), 1 + i) for i in range(4)]

        if KSTOP < 2:
            return _fin()
        with ExitStack() as e1:
            S.es = e1
            up_plan, dn_plan = [], []
            order = []
            for t in range(8):
                order += [("m", t), ("o", t)]
            order.append(("s", 0))
            for _ in order:
                up_plan += [(upb[0], upb[0].t.ap()[:, s]) for s in range(11)]
                dn_plan += [(dnb[0], dnb[0].t.ap()[:, oc]) for oc in range(KC)]
            ups = Stream(S, "ups", [128, 4, KC, 128], BF16, 2, up_plan, look=1)
            dns = Stream(S, "dns", [128, HC, 128], BF16, 2, dn_plan, look=1)
            winr = S.sb("winr", [128, 7, KC, 512], BF16)
            for sl in range(7):
                S.dma("sp", winr.t[:, sl], winb.t.ap()[:, sl], [winb], [winr], winr)
            xt = S.sb("xt", [128, KC, 512], F32)
            hb = S.sb("hb", [128, KC, 512], BF16)
            tmp = [S.sb("tmpn%d" % i, [128, 512], F32) for i in range(2)]
            rs = S.sb("rs", [128, 512], F32)
            hid = S.sb("hid", [128, HC, 512], BF16)
            sq = hid
            sgt = [S.sb("sgt%d" % i, [128, 512], F32) for i in range(1)]
            cst = [S.sb("cst%d" % i, [128, 2, 64], F32) for i in range(2)]
            rect = [S.sb("rect%d" % i, [128, 8, RECW], BF16) for i in range(2)]
            recto = [S.sb("recto%d" % i, [128, 2, RECW], BF16) for i in range(2)]
            for rr in rect:
                S.op("dve", lambda e, rr=rr: e.memset(rr.t[:], 0.0), [], [rr])
                S.op("dve", lambda e, rr=rr: e.memset(rr.t[:, 7, :], 1.0), [], [rr])
            for rr in recto:
                S.op("dve", lambda e, rr=rr: e.memset(rr.t[:], 0.0), [], [rr])
                S.op("dve", lambda e, rr=rr: e.memset(rr.t[:, 1, :], 1.0), [], [rr])
            w1 = S.sb("w1", [128, 512], F32)
            w2 = S.sb("w2", [128, 512], F32)
            w3, w4 = tmp[0], tmp[1]
            st8 = S.sb("st8", [128, 8], F32)
            o32 = [S.sb("o32_%d" % i, [128, 512], F32) for i in range(2)]
            tokb = [S.sb("tokb%d" % i, [128, 512], BF16) for i in range(2)]
            rkb = S.sb("rkb", [128, 512], BF16)
            rvb = S.sb("rvb", [128, 512], BF16)
            Lm = [S.sb("Lm%d" % i, [128, 512], F32) for i in range(4)]
            Lo = [S.sb("Lo%d" % i, [128, 512], F32) for i in range(1)]
            Sst = S.sb("Sst", [128, 512], F32)
            sc1 = S.sb("sc1", [128, 512], F32)
            sc2 = S.sb("sc2", [128, 512], F32)
            sc3 = S.sb("sc3", [128, 512], F32)
            sub = S.sb("sub", [128, 512], BF16)
            S.op("dve", lambda e: e.memset(Sst.t[:], 0.0), [], [Sst])
            cnt = dict(o32=0, tokb=0, rect=0, recto=0)

            def qknorm(ps, nt, gi, want32):
                S.op("act", lambda e: e.activation(out=w1.t[:nt], in_=ps.t[:nt], func=AF.Square), [ps], [w1])
                S.op("dve", lambda e: e.tensor_reduce(
                    out=st8.t[:nt], in_=w1.t[:nt].rearrange("p (h d) -> p h d", d=64), axis=AX.X, op=ALU.add),
                    [w1], [st8])
                S.op("act", lambda e: e.activation(out=st8.t[:nt], in_=st8.t[:nt], func=AF.Sqrt, bias=EPS,
                                                   scale=1.0 / 64), [st8], [st8])
                S.op("dve", lambda e: e.reciprocal(out=st8.t[:nt], in_=st8.t[:nt]), [st8], [st8])
                S.op("dve", lambda e: e.tensor_tensor(
                    out=w2.t[:nt].rearrange("p (h d) -> p h d", d=64),
                    in0=ps.t[:nt].rearrange("p (h d) -> p h d", d=64),
                    in1=st8.t[:nt].unsqueeze(2).to_broadcast([nt, 8, 64]), op=ALU.mult), [ps, st8], [w2])
                tb = tokb[cnt["tokb"] % 2]
                cnt["tokb"] += 1
                o = None
                if want32:
                    o = o32[cnt["o32"] % 2]
                    cnt["o32"] += 1
                    S.op("dve", lambda e: e.tensor_tensor(out=o.t[:nt], in0=w2.t[:nt], in1=c_qkg.t[:nt, gi, :],
                                                           op=ALU.mult), [w2, c_qkg], [o])
                S.op("dve", lambda e: e.tensor_tensor(out=tb.t[:nt], in0=w2.t[:nt], in1=c_qkg.t[:nt, gi, :],
                                                      op=ALU.mult), [w2, c_qkg], [tb])
                return tb, o

            def transpose4(tb, nt, dst, dstap):
                tp = tbank()
                for c in range(4):
                    S.op("pe", lambda e, c=c: e.transpose(out=tp.t[:, c * 128:c * 128 + nt],
                                                          in_=tb.t[:nt, c * 128:(c + 1) * 128],
                                                          identity=IDN[:nt, :nt]), [tb, c_bf], [tp], inc=(c == 3))
                S.op("act", lambda e: e.activation(
                    out=dstap.rearrange("p (c t) -> p c t", t=128)[:, :, :nt],
                    in_=tp.t[:, 0:512].rearrange("p (c t) -> p c t", t=128)[:, :, :nt], func=AF.Copy), [tp], [dst])

            def rope(ps, nt, cs, di, outb):
                v = ps.t[:nt].rearrange("p (h two d) -> p h two d", two=2, d=64)
                x1, x2 = v[:, :, 0, :], v[:, :, 1, :]
                cb = cs.t[:nt, 0:1, :].to_broadcast([nt, 4, 64])
                sb_ = cs.t[:nt, 1:2, :].to_broadcast([nt, 4, 64])
                a1 = w1.t[:nt, 0:256].rearrange("p (h d) -> p h d", d=64)
                a2 = w2.t[:nt, 0:256].rearrange("p (h d) -> p h d", d=64)
                a3 = w3.t[:nt, 0:256].rearrange("p (h d) -> p h d", d=64)
                a4 = w4.t[:nt, 0:256].rearrange("p (h d) -> p h d", d=64)
                S.op("dve", lambda e: e.tensor_tensor(out=a1, in0=x1, in1=cb, op=ALU.mult), [ps, cs], [w1])
                S.op("dve", lambda e: e.tensor_tensor(out=a2, in0=x2, in1=sb_, op=ALU.mult), [ps, cs], [w2])
                S.op("dve", lambda e: e.tensor_tensor(out=a3, in0=x1, in1=sb_, op=ALU.mult), [ps, cs], [w3])
                S.op("dve", lambda e: e.tensor_tensor(out=a4, in0=x2, in1=cb, op=ALU.mult), [ps, cs], [w4])
                r = w1.t[:nt, 256:512].rearrange("p (h d) -> p h d", d=64)
                r2 = w2.t[:nt, 256:512].rearrange("p (h d) -> p h d", d=64)
                S.op("dve", lambda e: e.tensor_tensor(out=r, in0=a1, in1=a2, op=ALU.subtract), [w1, w2], [w1])
                S.op("dve", lambda e: e.tensor_tensor(out=r2, in0=a3, in1=a4, op=ALU.add), [w3, w4], [w2])
                ov = outb.t[:nt].rearrange("p (h two d) -> p h two d", two=2, d=64)
                dc = c_dqk.t[:nt, di, :].unsqueeze(2).to_broadcast([nt, 4, 64])
                S.op("dve", lambda e: e.tensor_tensor(out=ov[:, :, 0, :], in0=r, in1=dc, op=ALU.mult),
                     [w1, c_dqk], [outb])
                S.op("dve", lambda e: e.tensor_tensor(out=ov[:, :, 1, :], in0=r2, in1=dc, op=ALU.mult),
                     [w2, c_dqk], [outb])

            deferred = []

            def flush():
                while deferred:
                    deferred.pop(0)()

            def proj(col0, nt, sl):
                ps = bank()
                for kc in range(KC):
                    S.op("pe", lambda e, kc=kc: e.matmul(ps.t[:nt, :], lhsT=hb.t[:, kc, col0:col0 + nt],
                                                        rhs=winr.t[:, sl, kc, :], start=(kc == 0),
                                                        stop=(kc == KC - 1)), [hb, winr], [ps], inc=(kc == KC - 1))
                flush()
                return ps

            def tm_block(kind, bi, col0, nt, tok0, cs):
                mine = kind != "o"
                if mine:
                    rt = rect[cnt["rect"] % 2]
                    cnt["rect"] += 1
                    kslot, vslot = 6, 7
                else:
                    rt = recto[cnt["recto"] % 2]
                    cnt["recto"] += 1
                    kslot, vslot = 0, 1
                if mine:
                    ps = proj(col0, nt, 0)
                    tb, _ = qknorm(ps, nt, 0, False)
                    deferred.append(lambda tb=tb: transpose4(tb, nt, rt, rt.t[:, 0, 0:512]))
                ps = proj(col0, nt, 1)
                tb, o = qknorm(ps, nt, 1, mine)
                if mine:
                    S.dma("pool", nk.t.ap()[tok0:tok0 + nt, :], o.t[:nt], [o], [nk], o)
                deferred.append(lambda tb=tb: transpose4(tb, nt, rt, rt.t[:, kslot, 0:512]))
                ps = proj(col0, nt, 2)
                if mine:
                    o = o32[cnt["o32"] % 2]
                    cnt["o32"] += 1
                    S.op("act", lambda e: e.activation(out=o.t[:nt], in_=ps.t[:nt], func=AF.Copy), [ps], [o])
                    S.dma("pool", nv.t.ap()[tok0:tok0 + nt, :], o.t[:nt], [o], [nv], o)
                S.op("dve", lambda e: e.tensor_copy(
                    out=rt.t[:nt, vslot, :].rearrange("p (h d) -> p h d", d=VW)[:, :, 0:64],
                    in_=ps.t[:nt].rearrange("p (h d) -> p h d", d=64)), [ps], [rt])
                if mine:
                    ps = proj(col0, nt, 3)
                    tb = tokb[cnt["tokb"] % 2]
                    cnt["tokb"] += 1
                    rope(ps, nt, cs, 0, tb)
                    deferred.append(lambda tb=tb: transpose4(tb, nt, rt, rt.t[:, 1, 0:512]))
                ps = proj(col0, nt, 4)
                rope(ps, nt, cs, 1, rkb)
                if mine:
                    deferred.append(lambda: transpose4(rkb, nt, rt, rt.t[:, 2, 0:512]))
                ps = proj(col0, nt, 5)
                S.op("act", lambda e: e.activation(out=rvb.t[:nt], in_=ps.t[:nt], func=AF.Copy), [ps], [rvb])
                if mine:
                    S.op("dve", lambda e: e.tensor_copy(out=rt.t[:nt, 3, 0:512], in_=rvb.t[:nt]), [rvb], [rt])
                    ps = proj(col0, nt, 6)
                    S.op("act", lambda e: e.activation(out=rt.t[:nt, 4, 0:512], in_=ps.t[:nt], func=AF.Silu),
                         [ps], [rt])
                def tail():
                    pl = bank()
                    for h in range(4):
                        S.op("pe", lambda e, h=h: e.matmul(pl.t[:, h * 128:(h + 1) * 128],
                                                           lhsT=rkb.t[:nt, h * 128:(h + 1) * 128],
                                                           rhs=rvb.t[:nt, h * 128:(h + 1) * 128],
                                                           start=(h == 0), stop=(h == 3), skip_group_check=True),
                             [rkb, rvb], [pl], inc=(h == 3))
                    if kind == "m":
                        S.op("act", lambda e: e.activation(out=Lm[bi % 4].t[:], in_=pl.t[:], func=AF.Copy), [pl], [Lm[bi % 4]])
                        S.dma("pool", recm.t.ap()[bi], rt.t[:], [rt], [recm], rt)
                    elif kind == "o":
                        S.op("act", lambda e: e.activation(out=Lo[0].t[:], in_=pl.t[:], func=AF.Copy), [pl], [Lo[0]])
                        S.dma("pool", reco.t.ap()[bi], rt.t[:], [rt], [reco], rt)
                        scan_step(bi)
                    else:
                        s = bi - NBLK
                        S.dma("sp", sc3.t[:], sret.t.ap()[:, s, :], [sret], [sc3], sc3)
                        S.op("dve", lambda e: e.tensor_copy(out=rt.t[:, 5, 0:512], in_=sc3.t[:]), [sc3], [rt])
                        S.op("dve", lambda e: e.tensor_tensor(out=sc1.t[:], in0=pl.t[:], in1=sc3.t[:], op=ALU.add),
                             [pl, sc3], [sc1])
                        S.op("dve", lambda e: e.tensor_tensor(out=sc1.t[:], in0=sc1.t[:], in1=c_gtab.t[:, 4, :],
                                                              op=ALU.mult), [sc1, c_gtab], [sc1])
                        S.dma("pool", ss_out.t.ap()[:, s, :], sc1.t[:], [sc1], [ss_out], sc1)
                        S.dma("pool", recm.t.ap()[bi], rt.t[:], [rt], [recm], rt)
                deferred.append(tail)

            def scan_step(i):
                lm, lo = Lm[i % 4], Lo[0]
                G128, G256, CM, CO = (c_gtab.t[:, k, :] for k in range(4))
                S.op("dve", lambda e: e.tensor_tensor(out=sc1.t[:], in0=Sst.t[:], in1=lo.t[:], op=ALU.add),
                     [Sst, lo], [sc1])
                S.op("dve", lambda e: e.tensor_tensor(out=sc1.t[:], in0=sc1.t[:], in1=G128, op=ALU.mult),
                     [sc1, c_gtab], [sc1])
                S.op("dve", lambda e: e.tensor_scalar(out=sc2.t[:], in0=Sst.t[:], scalar1=c_cab.t[:, 0:1],
                                                      scalar2=None, op0=ALU.mult), [Sst, c_cab], [sc2])
                S.op("dve", lambda e: e.scalar_tensor_tensor(out=sub.t[:], in0=sc1.t[:], scalar=c_cab.t[:, 1:2],
                                                             in1=sc2.t[:], op0=ALU.mult, op1=ALU.add),
                     [sc1, sc2, c_cab], [sub])
                S.dma("pool", suse.t.ap()[i], sub.t[:], [sub], [suse], sub)
                S.op("dve", lambda e: e.tensor_tensor(out=sc2.t[:], in0=lm.t[:], in1=CM, op=ALU.mult),
                     [lm, c_gtab], [sc2])
                S.op("dve", lambda e: e.tensor_tensor(out=sc3.t[:], in0=lo.t[:], in1=CO, op=ALU.mult),
                     [lo, c_gtab], [sc3])
                S.op("dve", lambda e: e.tensor_tensor(out=Sst.t[:], in0=Sst.t[:], in1=G256, op=ALU.mult),
                     [Sst, c_gtab], [Sst])
                S.op("dve", lambda e: e.tensor_tensor(out=sc2.t[:], in0=sc2.t[:], in1=sc3.t[:], op=ALU.add),
                     [sc2, sc3], [sc2])
                S.op("dve", lambda e: e.tensor_tensor(out=Sst.t[:], in0=Sst.t[:], in1=sc2.t[:], op=ALU.add),
                     [Sst, sc2], [Sst])

            for (kind, t) in order:
                N = 256 if kind == "s" else 512
                src = xo if kind == "o" else xm
                c0 = NBLK * 128 if kind == "s" else t * 512
                segs = SSEG if kind == "s" else PSEG(N)
                S.dma("sp", xt.t[:, :, :N], src.t.ap()[:, :, c0:c0 + N], [src], [xt], xt)
                if KSUB < 1:
                    return _fin()
                fm_norm(xt, 0, N, segs, hb, sq, tmp, rs)
                if KSUB < 2:
                    return _fin()
                ffn(xt, 0, N, segs, hb, hid, sgt, ups, dns)
                if KSUB < 3:
                    return _fin()
                if kind != "o" and not os.environ.get("NOX1"):
                    S.dma(os.environ.get("X1Q", "pool"), x1s.t.ap()[:, :, c0:c0 + N], xt.t[:, :, :N], [xt], [x1s], xt)
                fm_norm(xt, 1, N, segs, hb, sq, tmp, rs)
                nb = 4
                nt = 64 if kind == "s" else 128
                for b in range(nb):
                    bi = (NBLK + b) if kind == "s" else (t * 4 + b)
                    cs = cst[(cnt["rect"] + cnt["recto"]) % 2]
                    ctab, stab = (coso, sino) if kind == "o" else (cosm, sinm)
                    S.dma("sp", cs.t[:, 0, :], ctab.t.ap()[:, bi, :], [ctab], [cs], cs)
                    S.dma("sp", cs.t[:, 1, :], stab.t.ap()[:, bi, :], [stab], [cs], cs)
                    if KSUB < 4:
                        return _fin()
                    tm_block(kind, bi, b * nt, nt, c0 + b * nt, cs)
                    if KSUB < 5:
                        return _fin()
                flush()
            S.dma("pool", sp_out.t.ap(), Sst.t[:], [Sst], [sp_out], Sst)
            S.barrier()
            S.es = None


        if KSTOP < 3:
            return _fin()
        with ExitStack() as e2:
            S.es = e2
            m_bf = const("m_bf", mbf, [128, 3, 512], BF16)
            c_id32 = const("c_id32", id32, [128, 128], F32)
            c_rmask = const("c_rmask", rmask, [128, 128], BF16)
            KTm = S.sb("KTm", [128, NBLK, 512], BF16)
            KTo = S.sb("KTo", [128, NBLK, 512], BF16)
            Vm = S.sb("Vm", [128, NBLK, RECW], BF16)
            Vo = S.sb("Vo", [128, NBLK, RECW], BF16)
            for q in range(4):
                b0, b1 = q * 8, (q + 1) * 8
                S.dma("sp", KTm.t[:, b0:b1, :], recm.t.ap()[b0:b1, :, 6, 0:512].rearrange("b p f -> p b f"),
                      [recm], [KTm], KTm)
                S.dma("sp", KTo.t[:, b0:b1, :], reco.t.ap()[b0:b1, :, 0, 0:512].rearrange("b p f -> p b f"),
                      [reco], [KTo], KTo)
                S.dma("sp", Vm.t[:, b0:b1, :], recm.t.ap()[b0:b1, :, 7, :].rearrange("b p f -> p b f"),
                      [recm], [Vm], Vm)
                S.dma("sp", Vo.t[:, b0:b1, :], reco.t.ap()[b0:b1, :, 1, :].rearrange("b p f -> p b f"),
                      [reco], [Vo], Vo)
            ework = S.sb("ework", [128, 1024], F32)
            spw = [S.sb("spw%d" % i, [128, 1024], BF16) for i in range(2)]
            ww = [S.sb("ww%d" % i, [128, 1024], BF16) for i in range(2)]
            acc = S.sb("acc", [128, 8, 64], F32)
            tmpo = S.sb("tmpo", [128, 8, 64], F32)
            fst = S.sb("fst", [128, 8], F32)
            ft = S.sb("ft", [128, 8], F32)
            mixtok = S.sb("mixtok", [128, 1024], BF16)
            mixT = [S.sb("mixT%d" % i, [128, KC, 128], BF16) for i in range(2)]
            rec2 = [S.sb("rec2_%d" % i, [128, 8, RECW], BF16) for i in range(2)]
            sus = [S.sb("sus%d" % i, [128, 512], BF16) for i in range(2)]
            PT = S.sb("PT", [128, 512], BF16)
            g1 = S.sb("g1", [128, 512], F32)
            g2 = S.sb("g2", [128, 512], F32)
            s4 = S.sb("s4", [128, 4, 4], F32)
            stg = [S.sb("stg%d" % i, [128, 512], F32) for i in range(2)]
            ucnt = [0]
            Qz = [S.sb("Qz%d" % i, [128, 8, 128], BF16) for i in range(2)]
            for qq in Qz:
                S.op("dve", lambda e, qq=qq: e.memset(qq.t[:], 0.0), [], [qq])

            def fill_qz(qz, rt, nq):
                qv = qz.t[:].rearrange("p (c two) t -> p c two t", two=2)
                S.op("dve", lambda e: e.tensor_copy(
                    out=qv[0:64, :, 0, :nq],
                    in_=rt.t[0:64, 0, 0:512].rearrange("p (c t) -> p c t", t=128)[:, :, :nq]), [rt], [qz])
                S.op("dve", lambda e: e.tensor_copy(
                    out=qv[64:128, :, 1, :nq],
                    in_=rt.t[64:128, 0, 0:512].rearrange("p (c t) -> p c t", t=128)[:, :, :nq]), [rt], [qz])

            def uA(U):
                Q, hlist, nq, sides, nkeys, masks = U["Q"], U["hl"], U["nq"], U["sides"], U["nkeys"], U["masks"]
                u = U["idx"]
                zt = pz[u % 2]
                sp_ = spw[u % 2]
                nh = len(hlist)
                W = nh * nq
                ns = len(sides)
                for si, sd in enumerate(sides):
                    for hh, h in enumerate(hlist):
                        c, r0 = h // 2, (h % 2) * 64
                        last = (hh == nh - 1) and masks is None
                        S.op("pe", lambda e, si=si, sd=sd, hh=hh, c=c, r0=r0, h=h: e.matmul(
                            zt.t[:nkeys, si * 512 + hh * nq:si * 512 + (hh + 1) * nq],
                            lhsT=sd["kt"](c, r0), rhs=Q.t[:, h, 0:nq],
                            start=(hh == 0), stop=False, skip_group_check=True),
                            [sd["ktR"], Q], [zt], inc=last)
                    if masks is not None:
                        S.op("pe", lambda e, si=si: e.matmul(
                            zt.t[:nkeys, si * 512:si * 512 + W], lhsT=IDN[:nkeys, :nkeys], rhs=masks[si],
                            start=False, stop=False, skip_group_check=True), [c_bf, m_bf], [zt], inc=True)
                S.op("act", lambda e: e.activation(out=ework.t[:nkeys, 0:ns * 512], in_=zt.t[:nkeys, 0:ns * 512],
                                                   func=AF.Exp, scale=0.125), [zt], [ework])
                S.op("act", lambda e: e.activation(out=sp_.t[:nkeys, 0:ns * 512], in_=ework.t[:nkeys, 0:ns * 512],
                                                   func=AF.Ln, bias=1.0, scale=1.0), [ework], [sp_])

            def uB(U):
                hlist, nq, sides, nkeys, cross = U["hl"], U["nq"], U["sides"], U["nkeys"], U["cross"]
                u = U["idx"]
                zt = pz[u % 2]
                sp_, w_ = spw[u % 2], ww[u % 2]
                W = len(hlist) * nq
                ns = len(sides)
                for si in range(ns):
                    S.op("pe", lambda e, si=si: e.matmul(
                        zt.t[:nkeys, si * 512:si * 512 + W], lhsT=TRI[:nkeys, :nkeys],
                        rhs=sp_.t[:nkeys, si * 512:si * 512 + W], start=False, stop=(not cross),
                        skip_group_check=True), [c_bf, sp_], [zt], inc=(not cross and si == ns - 1))
                if cross:
                    S.op("pe", lambda e: e.matmul(zt.t[:, 0:512], lhsT=SELM, rhs=sp_.t[:, 512:1024],
                                                  start=False, stop=True, skip_group_check=True),
                         [c_bf, sp_], [zt], inc=False)
                    S.op("pe", lambda e: e.matmul(zt.t[:, 512:1024], lhsT=SELO, rhs=sp_.t[:, 0:512],
                                                  start=False, stop=True, skip_group_check=True),
                         [c_bf, sp_], [zt], inc=True)
                S.op("act", lambda e: e.activation(out=w_.t[:nkeys, 0:ns * 512], in_=zt.t[:nkeys, 0:ns * 512],
                                                   func=AF.Exp, scale=0.125), [zt], [w_])

            def uC(U):
                hlist, nq, sides, nkeys = U["hl"], U["nq"], U["sides"], U["nkeys"]
                u = U["idx"]
                w_ = ww[u % 2]
                nh = len(hlist)
                ns = len(sides)
                for g0 in range(0, nh, 4):
                    po = bank2()
                    firstmm = True
                    for hh in range(g0, g0 + 4):
                        h = hlist[hh]
                        for si, sd in enumerate(sides):
                            lastmm = (hh == g0 + 3) and (si == ns - 1)
                            S.op("pe", lambda e, si=si, sd=sd, hh=hh, h=h, firstmm=firstmm, lastmm=lastmm, po=po, g0=g0: e.matmul(
                                po.t[:nq, (hh - g0) * 65:(hh - g0 + 1) * 65],
                                lhsT=w_.t[:nkeys, si * 512 + hh * nq:si * 512 + (hh + 1) * nq], rhs=sd["v"](h),
                                start=firstmm, stop=lastmm, skip_group_check=True),
                                [w_, sd["vR"]], [po], inc=lastmm)
                            firstmm = False
                    pov = po.t[:nq, 0:4 * 65].rearrange("p (h d) -> p h d", d=65)
                    S.op("dve", lambda e, pov=pov, g0=g0: e.tensor_tensor(
                        out=tmpo.t[:nq, g0:g0 + 4, :], in0=pov[:, :, 0:64],
                        in1=fst.t[:nq, g0:g0 + 4].unsqueeze(2).to_broadcast([nq, 4, 64]), op=ALU.mult),
                        [po, fst], [tmpo])
                    S.op("dve", lambda e, g0=g0: e.tensor_tensor(out=acc.t[:nq, g0:g0 + 4, :], in0=acc.t[:nq, g0:g0 + 4, :],
                                                           in1=tmpo.t[:nq, g0:g0 + 4, :], op=ALU.add), [acc, tmpo], [acc])
                    S.op("dve", lambda e, pov=pov, g0=g0: e.tensor_tensor(out=ft.t[:nq, g0:g0 + 4], in0=pov[:, :, 64],
                                                          in1=fst.t[:nq, g0:g0 + 4], op=ALU.mult), [po, fst], [ft])
                    S.op("dve", lambda e, g0=g0: e.tensor_tensor(out=fst.t[:nq, g0:g0 + 4], in0=fst.t[:nq, g0:g0 + 4],
                                                          in1=ft.t[:nq, g0:g0 + 4], op=ALU.subtract), [fst, ft], [fst])

            def reset_acc(nq, nh):
                S.op("dve", lambda e: e.memset(fst.t[:], 1.0), [], [fst])
                S.op("dve", lambda e: e.memset(acc.t[:], 0.0), [], [acc])

            def retention(rt, suR, suap, nt):
                ps = bank2()
                for h in range(4):
                    S.op("pe", lambda e, h=h: e.matmul(ps.t[:nt, h * 128:h * 128 + nt],
                                                       lhsT=rt.t[:, 2, h * 128:h * 128 + nt],
                                                       rhs=rt.t[:, 1, h * 128:h * 128 + nt],
                                                       start=(h == 0), stop=(h == 3), skip_group_check=True),
                         [rt], [ps], inc=(h == 3))
                S.op("dve", lambda e: e.tensor_tensor(
                    out=PT.t[:nt].rearrange("p (h d) -> p h d", d=128)[:, :, :nt],
                    in0=ps.t[:nt].rearrange("p (h d) -> p h d", d=128)[:, :, :nt],
                    in1=c_rmask.t[:nt, :nt].unsqueeze(1).to_broadcast([nt, 4, nt]), op=ALU.mult),
                    [ps, c_rmask], [PT])
                po = bank2()
                for h in range(4):
                    S.op("pe", lambda e, h=h: e.matmul(po.t[:nt, h * 128:(h + 1) * 128],
                                                       lhsT=PT.t[:nt, h * 128:h * 128 + nt],
                                                       rhs=rt.t[:nt, 3, h * 128:(h + 1) * 128],
                                                       start=(h == 0), stop=False, skip_group_check=True),
                         [PT, rt], [po], inc=False)
                    S.op("pe", lambda e, h=h: e.matmul(po.t[:nt, h * 128:(h + 1) * 128],
                                                       lhsT=rt.t[:, 1, h * 128:h * 128 + nt],
                                                       rhs=suap[:, h * 128:(h + 1) * 128],
                                                       start=False, stop=True, skip_group_check=True),
                         [rt, suR], [po], inc=(h == 3))
                pov = po.t[:nt].rearrange("p (h d) -> p h d", d=128)
                S.op("dve", lambda e: e.tensor_reduce(out=s4.t[:nt, 0, :], in_=pov, axis=AX.X, op=ALU.add), [po], [s4])
                S.op("act", lambda e: e.activation(out=g1.t[:nt], in_=po.t[:nt], func=AF.Square), [po], [g1])
                S.op("dve", lambda e: e.tensor_reduce(out=s4.t[:nt, 1, :],
                                                      in_=g1.t[:nt].rearrange("p (h d) -> p h d", d=128),
                                                      axis=AX.X, op=ALU.add), [g1], [s4])
                S.op("dve", lambda e: e.tensor_scalar(out=s4.t[:nt, 0, :], in0=s4.t[:nt, 0, :], scalar1=1.0 / 128,
                                                      scalar2=None, op0=ALU.mult), [s4], [s4])
                S.op("dve", lambda e: e.tensor_tensor(out=s4.t[:nt, 2, :], in0=s4.t[:nt, 0, :], in1=s4.t[:nt, 0, :],
                                                      op=ALU.mult), [s4], [s4])
                S.op("dve", lambda e: e.scalar_tensor_tensor(out=s4.t[:nt, 1, :], in0=s4.t[:nt, 1, :],
                                                             scalar=1.0 / 128, in1=s4.t[:nt, 2, :],
                                                             op0=ALU.mult, op1=ALU.subtract), [s4], [s4])
                S.op("act", lambda e: e.activation(out=s4.t[:nt, 1, :], in_=s4.t[:nt, 1, :], func=AF.Sqrt,
                                                   bias=EPS, scale=1.0), [s4], [s4])
                S.op("dve", lambda e: e.reciprocal(out=s4.t[:nt, 1, :], in_=s4.t[:nt, 1, :]), [s4], [s4])
                S.op("dve", lambda e: e.tensor_tensor(
                    out=g1.t[:nt].rearrange("p (h d) -> p h d", d=128), in0=pov,
                    in1=s4.t[:nt, 0, :].unsqueeze(2).to_broadcast([nt, 4, 128]), op=ALU.subtract), [po, s4], [g1])
                S.op("dve", lambda e: e.tensor_tensor(
                    out=g2.t[:nt].rearrange("p (h d) -> p h d", d=128),
                    in0=g1.t[:nt].rearrange("p (h d) -> p h d", d=128),
                    in1=s4.t[:nt, 1, :].unsqueeze(2).to_broadcast([nt, 4, 128]), op=ALU.mult), [g1, s4], [g2])
                S.op("dve", lambda e: e.tensor_tensor(out=mixtok.t[:nt, 512:1024], in0=g2.t[:nt],
                                                       in1=rt.t[:nt, 4, 0:512], op=ALU.mult), [g2, rt], [mixtok])

            def emit_mix(bi, nt, tok0):
                tp = tbank()
                mt = mixT[bi % 2]
                for c in range(KC):
                    S.op("pe", lambda e, c=c: e.transpose(out=tp.t[:, c * 128:c * 128 + nt],
                                                          in_=mixtok.t[:nt, c * 128:(c + 1) * 128],
                                                          identity=IDN[:nt, :nt]), [mixtok, c_bf], [tp], inc=(c == KC - 1))
                S.op("act", lambda e: e.activation(
                    out=mt.t[:, :, :nt], in_=tp.t[:].rearrange("p (c t) -> p c t", t=128)[:, :, :nt], func=AF.Copy),
                    [tp], [mt])
                S.dma("pool", mixTs.t.ap()[:, :, tok0:tok0 + nt], mt.t[:, :, :nt], [mt], [mixTs], mt)

            units = []

            def mk_prompt_block(j):
                rt = rec2[j % 2]
                su = sus[j % 2]
                qz = Qz[j % 2]

                def preA():
                    S.dma("sp", rt.t[:], recm.t.ap()[j], [recm], [rt], rt)
                    S.dma("sp", su.t[:], suse.t.ap()[j], [suse], [su], su)
                    fill_qz(qz, rt, 128)

                for hg in range(2):
                    hl = [4 * hg + k for k in range(4)]
                    for i in range(j, -1, -1):
                        sides = [
                            dict(ktR=KTm, kt=lambda c, r0, i=i: KTm.t[:, i, c * 128:(c + 1) * 128],
                                 vR=Vm, v=lambda h, i=i: Vm.t[:, i, h * VW:h * VW + 65]),
                            dict(ktR=KTo, kt=lambda c, r0, i=i: KTo.t[:, i, c * 128:(c + 1) * 128],
                                 vR=Vo, v=lambda h, i=i: Vo.t[:, i, h * VW:h * VW + 65]),
                        ]
                        masks = [m_bf.t[:, 0, :], m_bf.t[:, 1, :]] if i == j else None
                        U = dict(Q=qz, hl=hl, nq=128, sides=sides, nkeys=128, masks=masks, cross=True)
                        if hg == 0 and i == j:
                            U["preA"] = preA
                        if i == j:
                            U["preD"] = lambda: reset_acc(128, 4)
                        if i == 0:
                            def postD(hg=hg):
                                S.op("act", lambda e: e.activation(
                                    out=mixtok.t[:, hg * 256:(hg + 1) * 256].rearrange("p (h d) -> p h d", d=64),
                                    in_=acc.t[:, 0:4, :], func=AF.Copy), [acc], [mixtok])
                                if hg == 1:
                                    retention(rt, su, su.t, 128)
                                    emit_mix(j, 128, j * 128)
                            U["postD"] = postD
                        units.append(U)

            def mk_sample_block(s_):
                rt = rec2[s_ % 2]
                qz = Qz[s_ % 2]

                def preA():
                    S.dma("sp", rt.t[:], recm.t.ap()[NBLK + s_], [recm], [rt], rt)
                    for kb in range(16):
                        sk = stg[0]
                        sv = stg[1]
                        S.dma("sp", sk.t[:], ck.t.ap()[s_, kb * 128:(kb + 1) * 128, :], [ck], [sk], sk)
                        S.dma("sp", sv.t[:], cv.t.ap()[s_, kb * 128:(kb + 1) * 128, :], [cv], [sv], sv)
                        tp32 = bank2()
                        for c in range(4):
                            S.op("pe", lambda e, c=c: e.transpose(out=tp32.t[:, c * 128:(c + 1) * 128],
                                                                  in_=sk.t[:, c * 128:(c + 1) * 128],
                                                                  identity=c_id32.t[:]), [sk, c_id32], [tp32],
                                 inc=(c == 3))
                        S.op("act", lambda e, kb=kb: e.activation(out=KTm.t[:, kb, :], in_=tp32.t[:], func=AF.Copy),
                             [tp32], [KTm])
                        S.op("dve", lambda e, kb=kb: e.tensor_copy(
                            out=Vm.t[:, kb, :].rearrange("p (h d) -> p h d", d=VW)[:, :, 0:64],
                            in_=sv.t[:].rearrange("p (h d) -> p h d", d=64)), [sv], [Vm])
                    fill_qz(qz, rt, 64)

                hl = list(range(8))
                sides = [dict(ktR=rt, kt=lambda c, r0: rt.t[:, 6, c * 128:c * 128 + 64],
                              vR=rt, v=lambda h: rt.t[:64, 7, h * VW:h * VW + 65])]
                units.append(dict(Q=qz, hl=hl, nq=64, sides=sides, nkeys=64, masks=[m_bf.t[:64, 2, :]],
                                  cross=False, preA=preA, preD=lambda: reset_acc(64, 8)))
                for kb in range(15, -1, -1):
                    sides = [dict(ktR=KTm, kt=lambda c, r0, kb=kb: KTm.t[:, kb, c * 128:(c + 1) * 128],
                                  vR=Vm, v=lambda h, kb=kb: Vm.t[:, kb, h * VW:h * VW + 65])]
                    U = dict(Q=qz, hl=hl, nq=64, sides=sides, nkeys=128, masks=None, cross=False)
                    if kb == 0:
                        def postD():
                            S.op("act", lambda e: e.activation(
                                out=mixtok.t[:64, 0:512].rearrange("p (h d) -> p h d", d=64),
                                in_=acc.t[:64, 0:8, :], func=AF.Copy), [acc], [mixtok])
                            retention(rt, rt, rt.t[:, 5, 0:512], 64)
                            emit_mix(s_, 64, NBLK * 128 + s_ * 64)
                        U["postD"] = postD
                    units.append(U)

            for j in range(NBLK):
                mk_prompt_block(j)
            npu = len(units)
            for s_ in range(NSB):
                mk_sample_block(s_)
            for k, U in enumerate(units):
                U["idx"] = k

            def run_units(lo, hi):
                def doA(t):
                    if "preA" in units[t]:
                        units[t]["preA"]()
                    uA(units[t])

                def doC(t):
                    U = units[t]
                    if "preD" in U:
                        U["preD"]()
                    uC(U)
                    if "postD" in U:
                        U["postD"]()

                prev = []
                t = lo
                while t < hi:
                    cur = [t] + ([t + 1] if t + 1 < hi else [])
                    for k, u in enumerate(cur):
                        if k < len(prev):
                            doC(prev[k])
                        doA(u)
                    for k in range(len(cur), len(prev)):
                        doC(prev[k])
                    for u in cur:
                        uB(units[u])
                    prev = cur
                    t += 2
                for u in prev:
                    doC(u)

            run_units(0, npu)
            run_units(npu, len(units))
            S.barrier()
            S.es = None

        if KSTOP < 4:
            return _fin()
        with ExitStack() as e3:
            S.es = e3
            up_plan, dn_plan = [], []
            for _ in range(9):
                up_plan += [(upb[1], upb[1].t.ap()[:, s]) for s in range(11)]
                dn_plan += [(dnb[1], dnb[1].t.ap()[:, oc]) for oc in range(KC)]
            ups = Stream(S, "ups2", [128, 4, KC, 128], BF16, 3, up_plan)
            dns = Stream(S, "dns2", [128, HC, 128], BF16, 3, dn_plan)
            woutr = S.sb("woutr", [128, KC, KC, 128], BF16)
            S.dma("sp", woutr.t[:], woutb.t.ap(), [woutb], [woutr], woutr)
            xts = [S.sb("xt2_%d" % i, [128, KC, 512], F32) for i in range(2)]
            mxs = [S.sb("mx%d" % i, [128, KC, 512], BF16) for i in range(2)]
            hb = S.sb("hb2", [128, KC, 512], BF16)
            hid = S.sb("hid2", [128, HC, 512], BF16)
            tmp = [S.sb("tmp2_%d" % i, [128, 512], F32) for i in range(2)]
            rs = S.sb("rs2", [128, 512], F32)
            sgt = [S.sb("sgt2", [128, 512], F32)]
            for t in range(9):
                N = 256 if t == 8 else 512
                c0 = t * 512
                segs = SSEG if t == 8 else PSEG(N)
                xt, mx = xts[t % 2], mxs[t % 2]
                S.dma("sp", xt.t[:, :, :N], x1s.t.ap()[:, :, c0:c0 + N], [x1s], [xt], xt)
                S.dma("sp", mx.t[:, :, :N], mixTs.t.ap()[:, :, c0:c0 + N], [mixTs], [mx], mx)
                for oc in range(KC):
                    ps = bank()
                    for kc in range(KC):
                        S.op("pe", lambda e, kc=kc, oc=oc: e.matmul(ps.t[:, :N], lhsT=woutr.t[:, oc, kc, :],
                                                                    rhs=mx.t[:, kc, :N], start=(kc == 0),
                                                                    stop=(kc == KC - 1)),
                             [woutr, mx], [ps], inc=(kc == KC - 1))
                    for (a, b, sq_) in segs:
                        S.op("dve", lambda e, a=a, b=b, sq_=sq_, oc=oc: e.scalar_tensor_tensor(
                            out=xt.t[:, oc, a:b], in0=ps.t[:, a:b], scalar=HG.t[:, 1, oc, sq_:sq_ + 1],
                            in1=xt.t[:, oc, a:b], op0=ALU.mult, op1=ALU.add), [ps, HG, xt], [xt])
                fm_norm(xt, 2, N, segs, hb, hid, tmp, rs)
                ffn(xt, 2, N, segs, hb, hid, sgt, ups, dns)
                S.dma("pool", yT.t.ap()[:, :, c0:c0 + N], xt.t[:, :, :N], [xt], [yT], xt)
            S.barrier()
            S.es = None

        S.finish([yT, nk, nv, sp_out, ss_out, recm, reco, suse, x1s, mixTs])
    print("semaphores used:", S.nsem, "nops", S.nops, "instr counts:", {k: v["n"] for k, v in S.E.items()})
    nc._oplog = S.log
    return nc


def _fm(x):
    T = x.shape[0]
    return np.ascontiguousarray(x.reshape(T, KC, 128).transpose(2, 1, 0))


def _consts():
    lg = np.log1p(-np.exp2(-5.0 - np.arange(4, dtype=np.float32))).astype(np.float32)
    idx = np.arange(128, dtype=np.float32)
    dq = np.exp((idx + 1.0)[:, None] * lg[None, :]).astype(np.float32)
    dk = (np.exp(-(idx + 1.0)[:, None] * lg[None, :]) * np.float32(128 ** -0.5)).astype(np.float32)
    dqk = np.stack([dq, dk], 1).astype(np.float32)
    g = lambda n: np.repeat(np.exp(np.float32(n) * lg).astype(np.float32), 128)[None, :].repeat(128, 0)
    half = 32
    inv_freq = (10000.0 ** (-np.arange(64, dtype=np.float32) / 64)).astype(np.float32)
    return lg, dqk, g, inv_freq


def _rope_tab(pos, inv_freq):
    ang = pos.astype(np.float32)[:, None] * inv_freq[None, :]
    return np.cos(ang).astype(np.float32), np.sin(ang).astype(np.float32)


_NC_CACHE = {}


def kernel(x_prompt, x_sample, cache_sb_k, cache_sb_v, state_ret, c_prompt, c_sample,
           w_ada, b_ada, norm_ffn1, norm_mix, norm_ffn2, ffn1_w_up, ffn1_w_down,
           w_in, sb_q_gain, sb_k_gain, w_out, ffn2_w_up, ffn2_w_down):
    f = lambda a: np.asarray(a, dtype=np.float32)
    x_prompt, x_sample = f(x_prompt), f(x_sample)
    lg, dqk, g, inv_freq = _consts()

    def kmaj(w, cols):
        return np.ascontiguousarray(w.reshape(KC, 128, cols).transpose(1, 0, 2))

    def up_l(w):
        a = w.reshape(KC, 128, 2, 11, 2, 128)
        a = a.transpose(1, 3, 2, 4, 0, 5)
        return np.ascontiguousarray(a.reshape(128, 11, 4, KC, 128))

    def dn_l(w):
        a = w.reshape(HC, 128, KC, 128).transpose(1, 2, 0, 3)
        return np.ascontiguousarray(a)

    shared = {
        "wada": kmaj(f(w_ada)[0], 9 * D),
        "bada": np.ascontiguousarray(f(b_ada)[0].reshape(72, 128).T),
        "gains": np.ascontiguousarray(np.stack([f(norm_ffn1)[0], f(norm_mix)[0], f(norm_ffn2)[0]], 0)
                                      .reshape(3, KC, 128).transpose(2, 0, 1)),
        "up1": up_l(f(ffn1_w_up)[0]), "up2": up_l(f(ffn2_w_up)[0]),
        "dn1": dn_l(f(ffn1_w_down)[0]), "dn2": dn_l(f(ffn2_w_down)[0]),
        "win": np.ascontiguousarray(f(w_in)[0].reshape(KC, 128, 7, 512).transpose(1, 2, 0, 3)),
        "wout": np.ascontiguousarray(f(w_out)[0].reshape(KC, 128, KC, 128).transpose(1, 2, 0, 3)),
        "qkg": np.ascontiguousarray(np.stack([np.tile(f(sb_q_gain)[0], 8), np.tile(f(sb_k_gain)[0], 8)], 0)[None]
                                    .repeat(128, 0)),
        "dqk": dqk,
        "id32": np.eye(128, dtype=np.float32),
    }
    kk = np.arange(128)
    rmask = (kk[None, :] >= kk[:, None]).astype(np.float32)
    shared["rmask"] = rmask.astype(NPBF)
    tri = np.where(kk[:, None] >= kk[None, :], -8.0, 0.0).astype(np.float32)
    diag = np.where(kk[:, None] < kk[None, :], 0.0, NEG).astype(np.float32)
    ones = np.ones((128, 128), np.float32)
    smp = np.full((128, 64), NEG, np.float32)
    smp[:64] = diag[:64, :64]

    in_maps = []
    for c in range(8):
        s, r = c // 2, c % 2
        xs = x_prompt[s].reshape(64, 128, D)
        mine = xs[r::2].reshape(NBLK * 128, D)
        oth = xs[1 - r::2].reshape(NBLK * 128, D)
        smp_x = x_sample[4 * c:4 * c + 4].reshape(256, D)
        cA, cB = (1.0, 0.0) if r == 0 else (0.0, 1.0)
        posm = (np.arange(NBLK)[:, None] * 2 + r) * 128 + np.arange(128)[None, :]
        poso = (np.arange(NBLK)[:, None] * 2 + (1 - r)) * 128 + np.arange(128)[None, :]
        cm, sm = _rope_tab(posm.reshape(-1), inv_freq)
        co, so = _rope_tab(poso.reshape(-1), inv_freq)
        cs_, ss_ = _rope_tab(PAST + np.arange(64), inv_freq)
        cosm = np.zeros((128, NBLK + NSB, 64), np.float32)
        sinm = np.zeros((128, NBLK + NSB, 64), np.float32)
        cosm[:, :NBLK] = cm.reshape(NBLK, 128, 64).transpose(1, 0, 2)
        sinm[:, :NBLK] = sm.reshape(NBLK, 128, 64).transpose(1, 0, 2)
        cosm[:64, NBLK:] = cs_[:, None, :]
        sinm[:64, NBLK:] = ss_[:, None, :]
        gtab = np.stack([g(128), g(256), cA * g(256) + cB * g(128), cB * g(256) + cA * g(128), g(64)], 1)
        cbf = np.stack([tri, -8.0 * cA * ones, -8.0 * cB * ones, np.eye(128, dtype=np.float32), ones], 1)
        moth = np.full((128, 128), NEG if r == 0 else 0.0, np.float32)
        mbf = np.stack([np.tile(diag, (1, 4)), np.tile(moth, (1, 4)), np.tile(smp, (1, 8))], 1)
        m = dict(shared)
        m.update({
            "xm": np.ascontiguousarray(np.concatenate([_fm(mine), _fm(smp_x)], 2)),
            "xo": _fm(oth),
            "cT": np.ascontiguousarray(np.concatenate([f(c_prompt)[s:s + 1], f(c_sample)[4 * c:4 * c + 4]], 0)
                                       .reshape(5, KC, 128).transpose(2, 1, 0)),
            "cosm": cosm, "sinm": sinm,
            "coso": np.ascontiguousarray(co.reshape(NBLK, 128, 64).transpose(1, 0, 2)),
            "sino": np.ascontiguousarray(so.reshape(NBLK, 128, 64).transpose(1, 0, 2)),
            "gtab": np.ascontiguousarray(gtab.astype(np.float32)),
            "cab": np.tile(np.array([[cA, cB]], np.float32), (128, 1)),
            "cbf": np.ascontiguousarray(cbf).astype(NPBF),
            "mbf": np.ascontiguousarray(mbf).astype(NPBF),
            "ck": np.ascontiguousarray(f(cache_sb_k)[0, 4 * c:4 * c + 4].reshape(4, PAST, 512)),
            "cv": np.ascontiguousarray(f(cache_sb_v)[0, 4 * c:4 * c + 4].reshape(4, PAST, 512)),
            "sret": np.ascontiguousarray(f(state_ret)[0, 4 * c:4 * c + 4].transpose(2, 0, 1, 3).reshape(128, 4, 512)),
        })
        in_maps.append(m)

    if "nc" not in _NC_CACHE:
        _NC_CACHE["nc"] = build_nc()
    res = run_bass_kernel_spmd(_NC_CACHE["nc"], in_maps, core_ids=list(range(8)))

    y_p = np.zeros((4, 64, 128, D), np.float32)
    k_p = np.zeros((4, 64, 128, 512), np.float32)
    v_p = np.zeros((4, 64, 128, 512), np.float32)
    y_s = np.zeros((32, 64, D), np.float32)
    k_s = np.zeros((32, 64, 512), np.float32)
    v_s = np.zeros((32, 64, 512), np.float32)
    st_p = np.zeros((4, 4, 128, 128), np.float32)
    st_s = np.zeros((32, 4, 128, 128), np.float32)
    for c in range(8):
        s, r = c // 2, c % 2
        o = res.results[c]
        yt = np.asarray(o["yT"]).transpose(2, 1, 0).reshape(NTOK, D)
        y_p[s, r::2] = yt[:NBLK * 128].reshape(NBLK, 128, D)
        y_s[4 * c:4 * c + 4] = yt[NBLK * 128:].reshape(4, 64, D)
        nk_ = np.asarray(o["nk"])
        nv_ = np.asarray(o["nv"])
        k_p[s, r::2] = nk_[:NBLK * 128].reshape(NBLK, 128, 512)
        v_p[s, r::2] = nv_[:NBLK * 128].reshape(NBLK, 128, 512)
        k_s[4 * c:4 * c + 4] = nk_[NBLK * 128:].reshape(4, 64, 512)
        v_s[4 * c:4 * c + 4] = nv_[NBLK * 128:].reshape(4, 64, 512)
        if r == 0:
            st_p[s] = np.asarray(o["sp_out"]).reshape(128, 4, 128).transpose(1, 0, 2)
        st_s[4 * c:4 * c + 4] = np.asarray(o["ss_out"]).reshape(128, 4, 4, 128).transpose(1, 2, 0, 3)
    return (y_p.reshape(4, 8192, D), y_s,
            k_p.reshape(1, 4, 8192, 8, 64), v_p.reshape(1, 4, 8192, 8, 64), st_p.reshape(1, 4, 4, 128, 128),
            k_s.reshape(1, 32, 64, 8, 64), v_s.reshape(1, 32, 64, 8, 64), st_s.reshape(1, 32, 4, 128, 128))
```
